# Optimizing a Trainium2 kernel written in Bass

```python
import math
import jax, jax.numpy as jnp
from jax import lax
import numpy as np

D_MODEL = 4096
BATCH = 32
SEQ = 256
DEPTH = 2
DEC_BATCH = 2
DEC_SEQ = 2048
PAST_LEN = 512

GRID_W = 64
N_BRANCH = 4
W_BRANCH = D_MODEL // N_BRANCH
D_FF = 2 * D_MODEL
N_MOD = 9
CONV_W = 4
CHUNK = 128
W_RG = W_BRANCH
RG_BLOCKS = 8
RG_BW = W_RG // RG_BLOCKS
RG_C = 8.0
W_HY = W_BRANCH
HY_ORDER = 2
HY_CONV = 3
HY_BANDS = 16
HY_EMB = 2 * HY_BANDS + 1
HY_FH = 64
HY_NF = HY_ORDER * 2 * W_HY
SSD_DI = W_BRANCH
SSD_P = 64
SSD_H = SSD_DI // SSD_P
SSD_N = 128
SSD_G = 2
SSD_CONV_CH = SSD_DI + 2 * SSD_G * SSD_N
W_ML = W_BRANCH
ML_NH = 4
ML_DH = W_ML // ML_NH
ROPE_BASE = 10000.0

ALPHA = (2 * DEPTH) ** 0.25
BETA = (8 * DEPTH) ** -0.25

IN_SIZES = (W_RG, W_RG, 3 * W_HY, SSD_DI, SSD_CONV_CH, 2 * SSD_H, 4 * W_ML, 4 * ML_NH, N_BRANCH * D_MODEL)
IN_OFFSETS = tuple(int(s) for s in np.cumsum(IN_SIZES)[:-1])
N_IN = int(sum(IN_SIZES))

kernel_name = "hybrid_bidir_diffusion_trunk_step"

F32 = jnp.float32


def layer_norm(x, g, b, eps=1e-5):
    xf = x.astype(F32)
    mu = jnp.mean(xf, axis=-1, keepdims=True)
    var = jnp.mean(jnp.square(xf - mu), axis=-1, keepdims=True)
    return ((xf - mu) * lax.rsqrt(var + eps)).astype(x.dtype) * g + b


def rms_norm(x, w, eps=1e-6):
    xf = x.astype(F32)
    return (xf * lax.rsqrt(jnp.mean(xf * xf, axis=-1, keepdims=True) + eps)).astype(x.dtype) * w


def head_norm(h, w, eps=1e-5):
    hf = h.astype(F32)
    mu = jnp.mean(hf, axis=-1, keepdims=True)
    var = jnp.mean(jnp.square(hf - mu), axis=-1, keepdims=True)
    return ((hf - mu) * lax.rsqrt(var + eps)).astype(h.dtype) * w.reshape(ML_NH, ML_DH)


def dw_conv(x, w, b, pad_l, pad_r):
    y = lax.conv_general_dilated(x, w[:, None, :], window_strides=(1,), padding=[(pad_l, pad_r)],
                                 dimension_numbers=('NWC', 'WIO', 'NWC'), feature_group_count=x.shape[-1])
    return y + b


def swiglu(h, wg, wu, wd):
    return (jax.nn.silu(h @ wg) * (h @ wu)) @ wd


def _rev(t):
    return jnp.flip(t, axis=1)


def _chunks(t):
    bn, L = t.shape[:2]
    return jnp.moveaxis(t.reshape((bn, L // CHUNK, CHUNK) + t.shape[2:]), 1, 0)


def _unchunk(t):
    t = jnp.moveaxis(t, 0, 1)
    return t.reshape((t.shape[0], t.shape[1] * t.shape[2]) + t.shape[3:])


def _lin_combine(e1, e2):
    a1, b1 = e1
    a2, b2 = e2
    return a1 * a2, a2 * b1 + b2


def rglru_scan(x, gw, gb, lam, h0):
    bn, L, W = x.shape
    g = jnp.einsum('blnc,gncd->gblnd', x.reshape(bn, L, RG_BLOCKS, RG_BW), gw).reshape(2, bn, L, W)
    g = g + gb[:, None, None, :]
    r, i = jax.nn.sigmoid(g[0]), jax.nn.sigmoid(g[1])
    log_a = -RG_C * r * jax.nn.softplus(-lam)
    a = jnp.exp(log_a)
    u = jnp.sqrt(-jnp.expm1(2.0 * log_a)) * (i * x)
    a_cum, b_cum = lax.associative_scan(_lin_combine, (a, u), axis=1)
    h = a_cum * h0[:, None, :] + b_cum
    return h, h[:, -1]


def hyena_mixer(u, lp):
    L = u.shape[1]
    u = dw_conv(u, lp['hy_conv_w'], lp['hy_conv_b'], 1, 1)
    v, x1, x2 = jnp.split(u, 3, axis=-1)
    t = jnp.arange(L, dtype=F32) / L
    ang = 2.0 * math.pi * t[:, None] * jnp.arange(1, HY_BANDS + 1, dtype=F32)[None, :]
    emb = jnp.concatenate([t[:, None], jnp.cos(ang), jnp.sin(ang)], axis=-1).astype(u.dtype)
    f = jnp.sin(emb @ lp['hy_w1'] + lp['hy_b1'])
    f = jnp.sin(f @ lp['hy_w2'] + lp['hy_b2'])
    f = (f @ lp['hy_w3']) * jnp.exp(-lp['hy_decay'][None, :] * t[:, None].astype(u.dtype))
    f = f.astype(F32).reshape(L, HY_ORDER, 2, W_HY)
    z = v
    for o, gate in enumerate((x1, x2)):
        hf, hb = f[:, o, 0], f[:, o, 1]
        filt = jnp.concatenate([hf, jnp.zeros_like(hf[:1]), jnp.flip(hb[1:], axis=0)], axis=0)
        filt = filt / (jnp.sum(jnp.abs(filt), axis=0, keepdims=True) + 1e-6)
        ff = jnp.fft.rfft(filt, axis=0)
        zf = jnp.fft.rfft(z.astype(F32), n=2 * L, axis=1)
        zc = jnp.fft.irfft(zf * ff[None], n=2 * L, axis=1)[:, :L].astype(u.dtype)
        z = gate * (zc + lp['hy_bias'][o] * z)
    return z


def ssd_scan(x, dt, A, bm, cm, S0):
    mask = jnp.tril(jnp.ones((CHUNK, CHUNK), dtype=bool))[None, :, :, None]

    def step(S, inp):
        xc, dtc, bc, cc = inp
        cs = jnp.cumsum(dtc * A, axis=1)
        seg = cs[:, :, None, :] - cs[:, None, :, :]
        decay = jnp.exp(jnp.where(mask, seg, -jnp.inf))
        scores = jnp.einsum('bthn,bshn->btsh', cc, bc) * decay
        y = jnp.einsum('btsh,bsh,bshp->bthp', scores, dtc, xc)
        y = y + jnp.einsum('bthn,bhpn->bthp', cc, S) * jnp.exp(cs)[..., None]
        w_end = jnp.exp(cs[:, -1:, :] - cs) * dtc
        S_new = jnp.exp(cs[:, -1, :])[:, :, None, None] * S + jnp.einsum('bshn,bsh,bshp->bhpn', bc, w_end, xc)
        return S_new.astype(S.dtype), y.astype(xc.dtype)

    S_last, ys = lax.scan(step, S0, tuple(_chunks(t) for t in (x, dt, bm, cm)))
    return _unchunk(ys), S_last


def mlstm_scan(q, k, v, i_pre, f_pre, C0, n0, m0):
    mask = jnp.tril(jnp.ones((CHUNK, CHUNK), dtype=bool))[None, :, :, None]

    def step(carry, inp):
        C, n, m = carry
        qc, kc, vc, ic, fc = (t.astype(F32) for t in inp)
        Cf, nf, mf = C.astype(F32), n.astype(F32), m.astype(F32)
        b = jnp.cumsum(jax.nn.log_sigmoid(fc), axis=1)
        dmat = jnp.where(mask, b[:, :, None, :] - b[:, None, :, :] + ic[:, None, :, :], -jnp.inf)
        g0 = b + mf[:, None, :]
        mt = jnp.maximum(g0, jnp.max(dmat, axis=2))
        w = jnp.exp(dmat - mt[:, :, None, :])
        w0 = jnp.exp(g0 - mt)
        s = jnp.einsum('bthd,bshd->btsh', qc, kc) * w
        num = jnp.einsum('btsh,bshd->bthd', s, vc) + w0[..., None] * jnp.einsum('bhed,bthd->bthe', Cf, qc)
        den = jnp.sum(s, axis=2) + w0 * jnp.einsum('bhd,bthd->bth', nf, qc)
        h = num / jnp.maximum(jnp.abs(den), jnp.exp(-mt))[..., None]
        b_end = b[:, -1]
        d_end = b_end[:, None, :] - b + ic
        g0_end = b_end + mf
        m_new = jnp.maximum(g0_end, jnp.max(d_end, axis=1))
        we = jnp.exp(d_end - m_new[:, None, :])
        w0e = jnp.exp(g0_end - m_new)
        C_new = w0e[..., None, None] * Cf + jnp.einsum('bsh,bshe,bshd->bhed', we, vc, kc)
        n_new = w0e[..., None] * nf + jnp.einsum('bsh,bshd->bhd', we, kc)
        return (C_new.astype(C.dtype), n_new.astype(n.dtype), m_new.astype(m.dtype)), h.astype(inp[0].dtype)

    (C, n, m), hs = lax.scan(step, (C0, n0, m0), tuple(_chunks(t) for t in (q, k, v, i_pre, f_pre)))
    return _unchunk(hs), (C, n, m)


def rope_2d(t, rows, cols):
    half = t.shape[-1] // 2

    def rot(u, pos):
        nf = u.shape[-1] // 2
        freqs = ROPE_BASE ** (-jnp.arange(nf, dtype=F32) / nf)
        ang = pos.astype(F32)[:, None] * freqs[None, :]
        cos = jnp.cos(ang)[None, :, None, :].astype(u.dtype)
        sin = jnp.sin(ang)[None, :, None, :].astype(u.dtype)
        u1, u2 = u[..., :nf], u[..., nf:]
        return jnp.concatenate([u1 * cos - u2 * sin, u1 * sin + u2 * cos], axis=-1)

    return jnp.concatenate([rot(t[..., :half], rows), rot(t[..., half:], cols)], axis=-1)


def token_mixers(h, st0, lp, grid):
    h0_rg, S0_ssd, C0, n0, m0 = st0
    bn, L, _ = h.shape
    proj = h @ lp['w_in']
    rg_x, rg_y, hy_u, ssd_z, ssd_xbc, ssd_dt, ml_qkvo, ml_if, merge_g = jnp.split(proj, IN_OFFSETS, axis=-1)

    xa = dw_conv(rg_x, lp['rg_conv_w'], lp['rg_conv_b'], 2, 1)
    ha_f, hl_f = rglru_scan(xa, lp['rg_gate_w'][0], lp['rg_gate_b'][0], lp['rg_lambda'][0], h0_rg[:, 0])
    ha_b, hl_b = rglru_scan(_rev(xa), lp['rg_gate_w'][1], lp['rg_gate_b'][1], lp['rg_lambda'][1], h0_rg[:, 1])
    out_a = (ha_f + _rev(ha_b)) * jax.nn.gelu(rg_y)

    out_b = hyena_mixer(hy_u, lp)

    xbc = jax.nn.silu(dw_conv(ssd_xbc, lp['ssd_conv_w'], lp['ssd_conv_b'], 2, 1))
    xs, bm, cm = jnp.split(xbc, [SSD_DI, SSD_DI + SSD_G * SSD_N], axis=-1)
    xs = xs.reshape(bn, L, SSD_H, SSD_P)
    bm = jnp.repeat(bm.reshape(bn, L, SSD_G, SSD_N), SSD_H // SSD_G, axis=2)
    cm = jnp.repeat(cm.reshape(bn, L, SSD_G, SSD_N), SSD_H // SSD_G, axis=2)
    dt = jax.nn.softplus(ssd_dt.reshape(bn, L, 2, SSD_H) + lp['ssd_dt_bias'])
    A = -jnp.exp(lp['ssd_A_log'])
    yc_f, S_f = ssd_scan(xs, dt[:, :, 0], A[0], bm, cm, S0_ssd[:, 0])
    yc_b, S_b = ssd_scan(_rev(xs), _rev(dt[:, :, 1]), A[1], _rev(bm), _rev(cm), S0_ssd[:, 1])
    yc = yc_f + _rev(yc_b) + lp['ssd_D'][:, None] * xs
    out_c = rms_norm(yc.reshape(bn, L, SSD_DI) * jax.nn.silu(ssd_z), lp['ssd_norm_w'])

    q, k, v, o = jnp.split(ml_qkvo, 4, axis=-1)
    hd = (bn, L, ML_NH, ML_DH)
    q, k, v = q.reshape(hd), k.reshape(hd) * (ML_DH ** -0.5), v.reshape(hd)
    if grid is not None:
        q, k = rope_2d(q, *grid), rope_2d(k, *grid)
    gt = ml_if.reshape(bn, L, 2, 2, ML_NH) + lp['ml_gate_b']
    hd_f, (C_f, n_f, m_f) = mlstm_scan(q, k, v, gt[:, :, 0, 0], gt[:, :, 0, 1], C0[:, 0], n0[:, 0], m0[:, 0])
    hd_b, (C_b, n_b, m_b) = mlstm_scan(_rev(q), _rev(k), _rev(v), _rev(gt[:, :, 1, 0]), _rev(gt[:, :, 1, 1]),
                                       C0[:, 1], n0[:, 1], m0[:, 1])
    out_d = jax.nn.sigmoid(o) * head_norm(hd_f + _rev(hd_b), lp['ml_norm_w']).reshape(bn, L, W_ML)

    gates = jnp.split(merge_g, N_BRANCH, axis=-1)
    merged = 0.0
    for j, br in enumerate((out_a, out_b, out_c, out_d)):
        merged = merged + jax.nn.sigmoid(gates[j]) * (br @ lp['branch_w'][j])
    out = merged @ lp['mix_out']
    states = (jnp.stack([hl_f, hl_b], axis=1), jnp.stack([S_f, S_b], axis=1), jnp.stack([C_f, C_b], axis=1),
              jnp.stack([n_f, n_b], axis=1), jnp.stack([m_f, m_b], axis=1))
    return out, states


def trunk_layer(x, mod, st0, lp, grid):
    m = [mod[:, j, None, :] for j in range(N_MOD)]
    x = layer_norm(ALPHA * x + 0.5 * m[2] * swiglu(x * (1 + m[1]) + m[0], lp['ffn_wg'][0], lp['ffn_wu'][0], lp['ffn_wd'][0]),
                   lp['ln_g'][0], lp['ln_b'][0])
    mix, st = token_mixers(x * (1 + m[4]) + m[3], st0, lp, grid)
    x = layer_norm(ALPHA * x + m[5] * mix, lp['ln_g'][1], lp['ln_b'][1])
    x = layer_norm(ALPHA * x + 0.5 * m[8] * swiglu(x * (1 + m[7]) + m[6], lp['ffn_wg'][1], lp['ffn_wu'][1], lp['ffn_wd'][1]),
                   lp['ln_g'][2], lp['ln_b'][2])
    return x, st


def setup_inputs(seed: int = 0) -> dict:
    key = jax.random.key(seed)
    ks = iter(jax.random.split(key, 64))

    def nrm(shape, s):
        return jax.random.normal(next(ks), shape, jnp.float32) * s

    def unif(shape, lo, hi):
        return jax.random.uniform(next(ks), shape, jnp.float32, lo, hi)

    a_rg = unif((DEPTH, 2, W_RG), 0.9, 0.999)
    dt0 = jnp.exp(unif((DEPTH, 2, SSD_H), math.log(1e-3), math.log(1e-1)))
    return {
        "x_prompt": nrm((BATCH, SEQ, D_MODEL), 1.0),
        "x_sample": nrm((DEC_BATCH, DEC_SEQ, D_MODEL), 1.0),
        "state_rglru": nrm((DEC_BATCH, DEPTH, 2, W_RG), 0.5),
        "state_ssd": nrm((DEC_BATCH, DEPTH, 2, SSD_H, SSD_P, SSD_N), 0.3),
        "state_mlstm_C": nrm((DEC_BATCH, DEPTH, 2, ML_NH, ML_DH, ML_DH), 0.3),
        "state_mlstm_n": nrm((DEC_BATCH, DEPTH, 2, ML_NH, ML_DH), 0.3),
        "state_mlstm_m": unif((DEC_BATCH, DEPTH, 2, ML_NH), 0.0, 2.0),
        "c": nrm((DEC_BATCH, D_MODEL), 1.0),
        "c_ctx": nrm((D_MODEL,), 1.0),
        "ada_w": nrm((DEPTH, D_MODEL, N_MOD * D_MODEL), 0.5 * D_MODEL ** -0.5),
        "ada_b": nrm((DEPTH, N_MOD * D_MODEL), 0.02),
        "ln_g": 1.0 + nrm((DEPTH, 3, D_MODEL), 0.05),
        "ln_b": nrm((DEPTH, 3, D_MODEL), 0.02),
        "ffn_wg": nrm((DEPTH, 2, D_MODEL, D_FF), D_MODEL ** -0.5),
        "ffn_wu": nrm((DEPTH, 2, D_MODEL, D_FF), D_MODEL ** -0.5),
        "ffn_wd": nrm((DEPTH, 2, D_FF, D_MODEL), BETA * D_FF ** -0.5),
        "w_in": nrm((DEPTH, D_MODEL, N_IN), D_MODEL ** -0.5),
        "rg_conv_w": nrm((DEPTH, CONV_W, W_RG), CONV_W ** -0.5),
        "rg_conv_b": nrm((DEPTH, W_RG), 0.02),
        "rg_gate_w": nrm((DEPTH, 2, 2, RG_BLOCKS, RG_BW, RG_BW), RG_BW ** -0.5),
        "rg_gate_b": nrm((DEPTH, 2, 2, W_RG), 0.02),
        "rg_lambda": jnp.log(a_rg) - jnp.log1p(-a_rg),
        "hy_conv_w": nrm((DEPTH, HY_CONV, 3 * W_HY), HY_CONV ** -0.5),
        "hy_conv_b": nrm((DEPTH, 3 * W_HY), 0.02),
        "hy_w1": nrm((DEPTH, HY_EMB, HY_FH), HY_EMB ** -0.5),
        "hy_b1": nrm((DEPTH, HY_FH), 0.02),
        "hy_w2": nrm((DEPTH, HY_FH, HY_FH), HY_FH ** -0.5),
        "hy_b2": nrm((DEPTH, HY_FH), 0.02),
        "hy_w3": nrm((DEPTH, HY_FH, HY_NF), HY_FH ** -0.5),
        "hy_decay": unif((DEPTH, HY_NF), 1.0, 20.0),
        "hy_bias": nrm((DEPTH, HY_ORDER, W_HY), 0.1),
        "ssd_conv_w": nrm((DEPTH, CONV_W, SSD_CONV_CH), CONV_W ** -0.5),
        "ssd_conv_b": nrm((DEPTH, SSD_CONV_CH), 0.02),
        "ssd_dt_bias": dt0 + jnp.log(-jnp.expm1(-dt0)),
        "ssd_A_log": jnp.log(unif((DEPTH, 2, SSD_H), 1.0, 16.0)),
        "ssd_D": 1.0 + nrm((DEPTH, SSD_H), 0.1),
        "ssd_norm_w": 1.0 + nrm((DEPTH, SSD_DI), 0.05),
        "ml_gate_b": jnp.concatenate([nrm((DEPTH, 2, 1, ML_NH), 0.1), unif((DEPTH, 2, 1, ML_NH), 3.0, 6.0)], axis=2),
        "ml_norm_w": 1.0 + nrm((DEPTH, W_ML), 0.05),
        "branch_w": nrm((DEPTH, N_BRANCH, W_BRANCH, D_MODEL), W_BRANCH ** -0.5),
        "mix_out": nrm((DEPTH, D_MODEL, D_MODEL), BETA * D_MODEL ** -0.5),
    }


def reference(x_prompt, x_sample, state_rglru, state_ssd, state_mlstm_C, state_mlstm_n, state_mlstm_m, c, c_ctx,
              ada_w, ada_b, ln_g, ln_b, ffn_wg, ffn_wu, ffn_wd, w_in, rg_conv_w, rg_conv_b, rg_gate_w, rg_gate_b,
              rg_lambda, hy_conv_w, hy_conv_b, hy_w1, hy_b1, hy_w2, hy_b2, hy_w3, hy_decay, hy_bias,
              ssd_conv_w, ssd_conv_b, ssd_dt_bias, ssd_A_log, ssd_D, ssd_norm_w, ml_gate_b, ml_norm_w,
              branch_w, mix_out):
    bp = x_prompt.shape[0]
    bs, ls = x_sample.shape[:2]
    n_rows = ls // GRID_W
    rows = jnp.repeat(jnp.arange(n_rows), GRID_W)
    cols = jnp.arange(n_rows * GRID_W) % GRID_W
    dtp = x_prompt.dtype
    zero_st = (jnp.zeros((bp, 2, W_RG), dtp), jnp.zeros((bp, 2, SSD_H, SSD_P, SSD_N), dtp),
               jnp.zeros((bp, 2, ML_NH, ML_DH, ML_DH), dtp), jnp.zeros((bp, 2, ML_NH, ML_DH), dtp),
               jnp.zeros((bp, 2, ML_NH), dtp))
    yp, ys = x_prompt, x_sample
    new_rg, new_ssd, new_C, new_n, new_m = [], [], [], [], []
    for l in range(DEPTH):
        lp = dict(ln_g=ln_g[l], ln_b=ln_b[l], ffn_wg=ffn_wg[l], ffn_wu=ffn_wu[l], ffn_wd=ffn_wd[l], w_in=w_in[l],
                  rg_conv_w=rg_conv_w[l], rg_conv_b=rg_conv_b[l], rg_gate_w=rg_gate_w[l], rg_gate_b=rg_gate_b[l],
                  rg_lambda=rg_lambda[l], hy_conv_w=hy_conv_w[l], hy_conv_b=hy_conv_b[l], hy_w1=hy_w1[l],
                  hy_b1=hy_b1[l], hy_w2=hy_w2[l], hy_b2=hy_b2[l], hy_w3=hy_w3[l], hy_decay=hy_decay[l],
                  hy_bias=hy_bias[l], ssd_conv_w=ssd_conv_w[l], ssd_conv_b=ssd_conv_b[l],
                  ssd_dt_bias=ssd_dt_bias[l], ssd_A_log=ssd_A_log[l], ssd_D=ssd_D[l], ssd_norm_w=ssd_norm_w[l],
                  ml_gate_b=ml_gate_b[l], ml_norm_w=ml_norm_w[l], branch_w=branch_w[l], mix_out=mix_out[l])
        mod_ctx = (jax.nn.silu(c_ctx) @ ada_w[l] + ada_b[l]).reshape(1, N_MOD, D_MODEL)
        mod_lat = (jax.nn.silu(c) @ ada_w[l] + ada_b[l]).reshape(bs, N_MOD, D_MODEL)
        yp, st = trunk_layer(yp, mod_ctx, zero_st, lp, None)
        new_rg.append(st[0]); new_ssd.append(st[1]); new_C.append(st[2]); new_n.append(st[3]); new_m.append(st[4])
        cache_l = (state_rglru[:, l], state_ssd[:, l], state_mlstm_C[:, l], state_mlstm_n[:, l], state_mlstm_m[:, l])
        ys, _ = trunk_layer(ys, mod_lat, cache_l, lp, (rows, cols))
    out_rg = jnp.stack(new_rg, axis=1)
    out_ssd = jnp.stack(new_ssd, axis=1)
    out_C = jnp.stack(new_C, axis=1)
    out_n = jnp.stack(new_n, axis=1)
    out_m = jnp.stack(new_m, axis=1)
    return (yp, ys, out_rg, out_ssd, out_C, out_n, out_m)
```

```python
import numpy as np
import ml_dtypes
from contextlib import ExitStack
import concourse.bass as bass
import concourse.mybir as mybir
from concourse.bass_utils import run_bass_kernel_spmd

F32 = mybir.dt.float32
BF16 = mybir.dt.bfloat16
AF = mybir.ActivationFunctionType
ALU = mybir.AluOpType
AX = mybir.AxisListType


class Res:
    __slots__ = ("name", "w", "r", "excl")

    def __init__(self, name="", excl=False):
        self.name = name
        self.w = None
        self.r = {}
        self.excl = excl


class DT:
    def __init__(self, ap, n=None):
        self.ap = ap
        self.res = [Res() for _ in range(n if n is not None else ap.shape[0])]

    def __getitem__(self, i):
        return self.ap[i]


class Ctx:
    def __init__(self, nc, es, n_dma_sems=40):
        self.nc = nc
        self.es = es
        self.eng = {"pe": nc.tensor, "act": nc.scalar, "dve": nc.vector, "pool": nc.gpsimd, "sp": nc.sync}
        self.sem = {e: es.enter_context(nc.semaphore("sem_" + e)) for e in ["pe", "act", "dve", "pool"]}
        self.cnt = {e: 0 for e in self.sem}
        self.dsem = [es.enter_context(nc.semaphore("dsem%d" % i)) for i in range(n_dma_sems)]
        self.dcnt = [0] * n_dma_sems
        self.dnext = 0
        self.seen = {e: {} for e in self.eng}
        self.n_ins = 0

    def _wait(self, e, tok):
        sem, key, val = tok
        if self.seen[e].get(key, 0) >= val:
            return
        self.eng[e].wait_ge(sem, val)
        self.seen[e][key] = val
        self.n_ins += 1

    def _deps(self, e, reads, writes):
        deps = {}

        def add(tok):
            if tok is None:
                return
            k = tok[1]
            if k not in deps or deps[k][2] < tok[2]:
                deps[k] = tok

        for r in reads:
            add(r.w)
            if r.excl:
                for k, t in r.r.items():
                    if k != e:
                        add(t)
        for w in writes:
            add(w.w)
            for t in w.r.values():
                add(t)
        for k, tok in deps.items():
            if e == "pe" and k == "pe":
                continue
            self._wait(e, tok)

    def _mark(self, tok, reads, writes):
        for r in reads:
            r.r[tok[1]] = tok
        for w in writes:
            w.w = tok
            w.r = {}

    def op(self, e, fn, reads=(), writes=()):
        self._deps(e, reads, writes)
        ins = fn()
        self.cnt[e] += 1
        ins.then_inc(self.sem[e], 1)
        tok = (self.sem[e], e, self.cnt[e])
        self._mark(tok, reads, writes)
        self.n_ins += 1
        return tok

    def dma(self, e, out, in_, reads=(), writes=()):
        i = self.dnext
        self.dnext = (self.dnext + 1) % len(self.dsem)
        key = "d%d" % i
        if self.dcnt[i] > 0:
            self._wait(e, (self.dsem[i], key, self.dcnt[i]))
        self._deps(e, reads, writes)
        ins = self.eng[e].dma_start(out=out, in_=in_)
        self.dcnt[i] += 16
        ins.then_inc(self.dsem[i], 16)
        tok = (self.dsem[i], key, self.dcnt[i])
        self._mark(tok, reads, writes)
        self.n_ins += 1
        return tok

    def barrier(self, engines=("pe", "act", "dve", "pool", "sp")):
        toks = [(self.sem[e], e, self.cnt[e]) for e in self.sem if self.cnt[e] > 0]
        toks += [(self.dsem[i], "d%d" % i, self.dcnt[i]) for i in range(len(self.dsem)) if self.dcnt[i] > 0]
        for e in engines:
            for t in toks:
                self._wait(e, t)

    def final(self):
        self.barrier(engines=("sp",))


class Cfg:
    D = 4096
    FF = 8192
    T = 512
    NMOD = 9
    DEPTH = 2
    ALPHA = (2 * 2) ** 0.25
    LN_EPS = 1e-5

    @property
    def KC(self):
        return self.D // 128

    @property
    def FC(self):
        return self.FF // 128


class Dense:
    def __init__(self, c, cfg, es):
        self.c, self.cfg = c, cfg
        nc = c.nc
        self.nc = nc
        KC, FC, T = cfg.KC, cfg.FC, cfg.T
        sb = lambda name, shape, dt: es.enter_context(nc.sbuf_tensor(name, shape, dt))
        self.HB = sb("HB", [128, KC, T], BF16); self.HBr = Res("HB")
        self.HID = sb("HID", [128, FC, T], BF16); self.HIDr = Res("HID")
        self.WSZ = 32 * 256
        self.WS = sb("WS", [128, 4, self.WSZ], BF16); self.WSr = [Res("WS%d" % i) for i in range(4)]
        self.wpos = 0
        self.NST = 4
        self.xst = [sb("xst%d" % i, [128, T], F32) for i in range(3)]; self.xstr = [Res() for _ in range(3)]
        self.xpos = 0
        self.st = [sb("st%d" % i, [128, T], F32) for i in range(self.NST)]; self.str_ = [Res() for _ in range(self.NST)]
        self.stpos = 0
        self.yst = [sb("yst%d" % i, [128, T], F32) for i in range(2)]; self.ystr = [Res() for _ in range(2)]
        self.ypos = 0
        self.ones_c = sb("ones_c", [128, 1], F32)
        self.ones_r = sb("ones_r", [1, 128], F32)
        self.constr = Res("const")
        self.rows = sb("rows", [1, 4, T], F32); self.rowsr = Res("rows")
        self.MB = sb("MB", [128, T], F32); self.RB = sb("RB", [128, T], F32); self.MBr = Res("MB")
        self.ps = [es.enter_context(nc.psum_tensor("ps%d" % i, [128, 512], F32)) for i in range(8)]
        self.psr = [Res("ps%d" % i, excl=True) for i in range(8)]
        self.pspos = 0
        c.op("dve", lambda: nc.vector.memset(self.ones_c[:], 1.0), writes=[self.constr])
        c.op("dve", lambda: nc.vector.memset(self.ones_r[:], 1.0), writes=[self.constr])

    def psum(self):
        i = self.pspos; self.pspos = (i + 1) % 6
        return self.ps[i], self.psr[i]

    def stage(self):
        i = self.stpos; self.stpos = (i + 1) % self.NST
        return self.st[i], self.str_[i]

    def xstage(self):
        i = self.xpos; self.xpos = (i + 1) % 3
        return self.xst[i], self.xstr[i]

    def ystage(self):
        i = self.ypos; self.ypos = (i + 1) % 2
        return self.yst[i], self.ystr[i]

    def wslot(self, n=1):
        if n == 2 and self.wpos % 2 == 1:
            self.wpos += 1
        i = self.wpos % 4
        self.wpos += n
        if n == 1:
            return self.WS[:, i, :], [self.WSr[i]]
        return self.WS[:, i:i + 2, :].rearrange("p a b -> p (a b)"), [self.WSr[i], self.WSr[i + 1]]

    def gemm(self, blocks, compute):
        c = self.c

        def issue(i):
            out = []
            for view, nsl in blocks[i]:
                kc, cols = view.shape[1], view.shape[2]
                flat, res = self.wslot(nsl)
                sv = flat[:, 0:kc * cols].rearrange("p (k n) -> p k n", k=kc)
                c.dma("pool", sv, view, writes=res)
                out.append((sv, res))
            return out

        pend = issue(0)
        for i in range(len(blocks)):
            cur = pend
            if i + 1 < len(blocks):
                pend = issue(i + 1)
            compute(i, cur)

    def mm(self, ps, psr, wsv, wres, col0, insb, inres, kc_n, T, ncols=128):
        nc, c = self.nc, self.c
        for kc in range(kc_n):
            c.op("pe", lambda kc=kc: nc.tensor.matmul(ps[0:ncols, 0:T], lhsT=wsv[:, kc, col0:col0 + ncols], rhs=insb[:, kc, 0:T],
                                                    start=(kc == 0), stop=(kc == kc_n - 1)),
                 reads=list(wres) + [inres], writes=[psr])

    def ffn_up(self, wg, wu):
        cfg, nc, c = self.cfg, self.nc, self.c
        KC, FC, T = cfg.KC, cfg.FC, cfg.T
        wgv = wg.rearrange("(k p) n -> p k n", p=128)
        wuv = wu.rearrange("(k p) n -> p k n", p=128)
        BC = 256
        nb = cfg.FF // BC
        blocks = [[(wgv[:, :, b * BC:(b + 1) * BC], 1), (wuv[:, :, b * BC:(b + 1) * BC], 1)] for b in range(nb)]

        def compute(b, cur):
            (gsv, gres), (usv, ures) = cur
            for j in range(BC // 128):
                f = b * (BC // 128) + j
                pg, pgr = self.psum(); pu, pur = self.psum()
                self.mm(pg, pgr, gsv, gres, j * 128, self.HB, self.HBr, KC, T)
                self.mm(pu, pur, usv, ures, j * 128, self.HB, self.HBr, KC, T)
                sg, sgr = self.stage()
                c.op("act", lambda: nc.scalar.activation(out=sg[:, 0:T], in_=pg[:, 0:T], func=AF.Silu), reads=[pgr], writes=[sgr])
                c.op("dve", lambda: nc.vector.tensor_tensor(out=self.HID[:, f, :], in0=sg[:, 0:T], in1=pu[:, 0:T], op=ALU.mult),
                     reads=[sgr, pur], writes=[self.HIDr])

        self.gemm(blocks, compute)

    def proj_res(self, w, insb, inres, kc_n, x_src, gs_col, ypre):
        cfg, nc, c = self.cfg, self.nc, self.c
        KC, T = cfg.KC, cfg.T
        wv = w.rearrange("(k p) n -> p k n", p=128)
        nsl = 2
        BC = (nsl * self.WSZ) // kc_n
        BC = min(BC, 512, cfg.D)
        nb = cfg.D // BC
        blocks = [[(wv[:, :, b * BC:(b + 1) * BC], nsl)] for b in range(nb)]
        pstat, pstatr = self.ps[6], self.psr[6]
        pstat2, pstat2r = self.ps[7], self.psr[7]

        def compute(b, cur):
            (sv, res), = cur
            for j in range(BC // 128):
                n = b * (BC // 128) + j
                xs, xsr = self.xstage()
                c.dma("sp", xs[:, 0:T], x_src[n], reads=[x_src.res[n]], writes=[xsr])
                ps, psr = self.psum()
                self.mm(ps, psr, sv, res, j * 128, insb, inres, kc_n, T)
                ys, ysr = self.ystage()
                c.op("dve", lambda: nc.vector.scalar_tensor_tensor(out=ys[:, 0:T], in0=ps[:, 0:T], scalar=gs_col(n), in1=xs[:, 0:T],
                                                                 op0=ALU.mult, op1=ALU.add), reads=[psr, xsr, self.constr], writes=[ysr])
                sq, sqr = self.stage()
                c.op("act", lambda: nc.scalar.activation(out=sq[:, 0:T], in_=ys[:, 0:T], func=AF.Square), reads=[ysr], writes=[sqr])
                c.dma("act", ypre[n], ys[:, 0:T], reads=[ysr], writes=[ypre.res[n]])
                first, last = (n == 0), (n == KC - 1)
                c.op("pe", lambda: nc.tensor.matmul(pstat[0:1, 0:T], lhsT=self.ones_c[:, 0:1], rhs=ys[:, 0:T], start=first, stop=last,
                                                  skip_group_check=True), reads=[ysr, self.constr], writes=[pstatr])
                c.op("pe", lambda: nc.tensor.matmul(pstat2[0:1, 0:T], lhsT=self.ones_c[:, 0:1], rhs=sq[:, 0:T], start=first, stop=last,
                                                  skip_group_check=True), reads=[sqr, self.constr], writes=[pstat2r])

        self.gemm(blocks, compute)

    def ln_pass(self, ypre, g_col, b_col, x_out, hs_col, ho_col):
        cfg, nc, c = self.cfg, self.nc, self.c
        KC, T, D = cfg.KC, cfg.T, cfg.D
        eps = cfg.LN_EPS / (cfg.ALPHA ** 2)
        rows = self.rows
        pstat, pstatr = self.ps[6], self.psr[6]
        pstat2, pstat2r = self.ps[7], self.psr[7]
        c.op("dve", lambda: nc.vector.tensor_scalar(out=rows[0:1, 0, :], in0=pstat[0:1, 0:T], scalar1=1.0 / D, scalar2=None, op0=ALU.mult),
             reads=[pstatr], writes=[self.rowsr])
        c.op("dve", lambda: nc.vector.tensor_scalar(out=rows[0:1, 1, :], in0=pstat2[0:1, 0:T], scalar1=1.0 / D, scalar2=None, op0=ALU.mult),
             reads=[pstat2r], writes=[self.rowsr])
        c.op("dve", lambda: nc.vector.tensor_tensor(out=rows[0:1, 2, :], in0=rows[0:1, 0, :], in1=rows[0:1, 0, :], op=ALU.mult),
             reads=[self.rowsr], writes=[self.rowsr])
        c.op("dve", lambda: nc.vector.scalar_tensor_tensor(out=rows[0:1, 2, :], in0=rows[0:1, 1, :], scalar=1.0, in1=rows[0:1, 2, :],
                                                         op0=ALU.mult, op1=ALU.subtract), reads=[self.rowsr], writes=[self.rowsr])
        c.op("dve", lambda: nc.vector.tensor_scalar(out=rows[0:1, 2, :], in0=rows[0:1, 2, :], scalar1=eps, scalar2=None, op0=ALU.add),
             reads=[self.rowsr], writes=[self.rowsr])
        c.op("act", lambda: nc.scalar.activation(out=rows[0:1, 2, :], in_=rows[0:1, 2, :], func=AF.Sqrt), reads=[self.rowsr], writes=[self.rowsr])
        c.op("dve", lambda: nc.vector.reciprocal(out=rows[0:1, 2, :], in_=rows[0:1, 2, :]), reads=[self.rowsr], writes=[self.rowsr])
        c.op("dve", lambda: nc.vector.scalar_tensor_tensor(out=rows[0:1, 3, :], in0=rows[0:1, 0, :], scalar=-1.0, in1=rows[0:1, 2, :],
                                                         op0=ALU.mult, op1=ALU.mult), reads=[self.rowsr], writes=[self.rowsr])
        pb, pbr = self.psum()
        c.op("pe", lambda: nc.tensor.matmul(pb[:, 0:T], lhsT=self.ones_r[0:1, :], rhs=rows[0:1, 2, :], start=True, stop=True),
             reads=[self.rowsr, self.constr], writes=[pbr])
        c.op("act", lambda: nc.scalar.copy(out=self.RB[:, 0:T], in_=pb[:, 0:T]), reads=[pbr], writes=[self.MBr])
        pb2, pb2r = self.psum()
        c.op("pe", lambda: nc.tensor.matmul(pb2[:, 0:T], lhsT=self.ones_r[0:1, :], rhs=rows[0:1, 3, :], start=True, stop=True),
             reads=[self.rowsr, self.constr], writes=[pb2r])
        c.op("act", lambda: nc.scalar.copy(out=self.MB[:, 0:T], in_=pb2[:, 0:T]), reads=[pb2r], writes=[self.MBr])
        for n in range(KC):
            ys, ysr = self.stage()
            c.dma("sp", ys[:, 0:T], ypre[n], reads=[ypre.res[n]], writes=[ysr])
            c.op("dve", lambda: nc.vector.tensor_tensor(out=ys[:, 0:T], in0=ys[:, 0:T], in1=self.RB[:, 0:T], op=ALU.mult),
                 reads=[ysr, self.MBr], writes=[ysr])
            c.op("dve", lambda: nc.vector.tensor_tensor(out=ys[:, 0:T], in0=ys[:, 0:T], in1=self.MB[:, 0:T], op=ALU.add),
                 reads=[ysr, self.MBr], writes=[ysr])
            xo, xor_ = self.ystage()
            c.op("dve", lambda: nc.vector.tensor_scalar(out=xo[:, 0:T], in0=ys[:, 0:T], scalar1=g_col(n), scalar2=b_col(n),
                                                      op0=ALU.mult, op1=ALU.add), reads=[ysr, self.constr], writes=[xor_])
            c.dma("act", x_out[n], xo[:, 0:T], reads=[xor_], writes=[x_out.res[n]])
            if hs_col is not None:
                c.op("dve", lambda: nc.vector.tensor_scalar(out=self.HB[:, n, :], in0=xo[:, 0:T], scalar1=hs_col(n), scalar2=ho_col(n),
                                                          op0=ALU.mult, op1=ALU.add), reads=[xor_, self.constr], writes=[self.HBr])

    def load_h(self, x_src, hs_col, ho_col):
        cfg, nc, c = self.cfg, self.nc, self.c
        for n in range(cfg.KC):
            xs, xsr = self.xstage()
            c.dma("sp", xs[:, 0:cfg.T], x_src[n], reads=[x_src.res[n]], writes=[xsr])
            c.op("dve", lambda: nc.vector.tensor_scalar(out=self.HB[:, n, :], in0=xs[:, 0:cfg.T], scalar1=hs_col(n), scalar2=ho_col(n),
                                                      op0=ALU.mult, op1=ALU.add), reads=[xsr, self.constr], writes=[self.HBr])

    def adaln(self, ada_w, adab_sb, sc_sb, scr, modT, modr):
        cfg, nc, c = self.cfg, self.nc, self.c
        KC = cfg.KC
        wv = ada_w.rearrange("(k p) n -> p k n", p=128)
        BC = min(512, (2 * self.WSZ) // KC)
        nb = (9 * cfg.D) // BC
        blocks = [[(wv[:, :, b * BC:(b + 1) * BC], 2)] for b in range(nb)]
        npb = BC // 128

        def compute(b, cur):
            (sv, res), = cur
            ps, psr = self.psum()
            for j in range(npb):
                for kc in range(KC):
                    c.op("pe", lambda kc=kc: nc.tensor.matmul(ps[:, 2 * j:2 * j + 2], lhsT=sv[:, kc, j * 128:(j + 1) * 128], rhs=sc_sb[:, kc, :],
                                                            start=(kc == 0), stop=(kc == KC - 1)), reads=list(res) + [scr], writes=[psr])
            for s in range(2):
                c.op("dve", lambda: nc.vector.tensor_tensor(out=modT[:, b * npb:(b + 1) * npb, s], in0=ps[:, s:2 * npb:2],
                                                          in1=adab_sb[:, b * npb:(b + 1) * npb], op=ALU.add), reads=[psr, self.constr], writes=[modr])

        self.gemm(blocks, compute)
        a = cfg.ALPHA
        for j, (mul, add) in enumerate([(1, 0), (1, 1), (0.5 / a, 0), (1, 0), (1, 1), (1 / a, 0), (1, 0), (1, 1), (0.5 / a, 0)]):
            if mul == 1 and add == 0:
                continue
            sl = modT[:, j * KC:(j + 1) * KC, :]
            c.op("dve", lambda: nc.vector.tensor_scalar(out=sl, in0=sl, scalar1=float(mul), scalar2=float(add), op0=ALU.mult, op1=ALU.add),
                 reads=[modr], writes=[modr])

    def proj_mixer(self, w_in, fm_specs, tm_specs, tok0):
        cfg, nc, c = self.cfg, self.nc, self.c
        KC, T = cfg.KC, cfg.T
        wv = w_in.rearrange("(k p) n -> p k n", p=128)
        BC = min(512, (2 * self.WSZ) // KC)
        blocks, meta = [], []
        for (col0, ncols, fm, ch0) in fm_specs:
            for b in range(0, ncols, BC):
                w = min(BC, ncols - b)
                blocks.append([(wv[:, :, col0 + b:col0 + b + w], 2)]); meta.append(("fm", fm, ch0 + b // 128, w))
        for (col0, ncols, tm, dcol0) in tm_specs:
            for b in range(0, ncols, BC):
                w = min(BC, ncols - b)
                blocks.append([(wv[:, :, col0 + b:col0 + b + w], 2)]); meta.append(("tm", tm, dcol0 + b, w))

        def compute(i, cur):
            (sv, res), = cur
            kind, dst, d0, w = meta[i]
            if kind == "fm":
                for j in range(w // 128):
                    ps, psr = self.psum()
                    self.mm(ps, psr, sv, res, j * 128, self.HB, self.HBr, KC, T)
                    st, str_ = self.stage()
                    c.op("act", lambda: nc.scalar.copy(out=st[:, 0:T], in_=ps[:, 0:T]), reads=[psr], writes=[str_])
                    c.dma("act", dst.ap[d0 + j, :, tok0:tok0 + T], st[:, 0:T], reads=[str_], writes=[dst.res[d0 + j]])
            else:
                for tc in range(T // 128):
                    ps, psr = self.psum()
                    for kc in range(KC):
                        c.op("pe", lambda kc=kc: nc.tensor.matmul(ps[:, 0:w], lhsT=self.HB[:, kc, tc * 128:(tc + 1) * 128], rhs=sv[:, kc, 0:w],
                                                                start=(kc == 0), stop=(kc == KC - 1)), reads=list(res) + [self.HBr], writes=[psr])
                    st, str_ = self.stage()
                    c.op("act", lambda: nc.scalar.copy(out=st[:, 0:w], in_=ps[:, 0:w]), reads=[psr], writes=[str_])
                    r0 = tok0 + tc * 128
                    c.dma("act", dst.ap[r0:r0 + 128, d0:d0 + w], st[:, 0:w], reads=[str_], writes=[dst.res[0]])

        self.gemm(blocks, compute)

    def merge(self, w_in, gcol0, branch_w, BR, BRr, MERGED, MERGEDr):
        cfg, nc, c = self.cfg, self.nc, self.c
        KC, T, D = cfg.KC, cfg.T, cfg.D
        WBC = branch_w.shape[1] // 128
        wv = w_in.rearrange("(k p) n -> p k n", p=128)
        BC = min(256, D)
        blocks = []
        for ng in range(D // BC):
            for j in range(4):
                bw = branch_w[j].rearrange("(k p) n -> p k n", p=128)
                blocks.append([(wv[:, :, gcol0 + j * D + ng * BC: gcol0 + j * D + (ng + 1) * BC], 1), (bw[:, :, ng * BC:(ng + 1) * BC], 1)])
        npb = BC // 128
        acc = [self.MB, self.RB]

        def compute(i, cur):
            ng, j = divmod(i, 4)
            (gsv, gres), (bsv, bres) = cur
            for q in range(npb):
                n = ng * npb + q
                pg, pgr = self.psum(); pp, ppr = self.psum()
                self.mm(pg, pgr, gsv, gres, q * 128, self.HB, self.HBr, KC, T)
                for kc in range(WBC):
                    c.op("pe", lambda kc=kc: nc.tensor.matmul(pp[:, 0:T], lhsT=bsv[:, kc, q * 128:(q + 1) * 128], rhs=BR[:, j * WBC + kc, 0:T],
                                                            start=(kc == 0), stop=(kc == WBC - 1)), reads=list(bres) + [BRr], writes=[ppr])
                sg, sgr = self.stage()
                c.op("act", lambda: nc.scalar.activation(out=sg[:, 0:T], in_=pg[:, 0:T], func=AF.Sigmoid), reads=[pgr], writes=[sgr])
                a = acc[q]
                if j == 0:
                    c.op("dve", lambda: nc.vector.tensor_tensor(out=a[:, 0:T], in0=sg[:, 0:T], in1=pp[:, 0:T], op=ALU.mult),
                         reads=[sgr, ppr], writes=[self.MBr])
                else:
                    c.op("dve", lambda: nc.vector.tensor_tensor(out=sg[:, 0:T], in0=sg[:, 0:T], in1=pp[:, 0:T], op=ALU.mult),
                         reads=[sgr, ppr], writes=[sgr])
                    if j < 3:
                        c.op("dve", lambda: nc.vector.tensor_tensor(out=a[:, 0:T], in0=a[:, 0:T], in1=sg[:, 0:T], op=ALU.add),
                             reads=[sgr, self.MBr], writes=[self.MBr])
                    else:
                        c.op("dve", lambda: nc.vector.tensor_tensor(out=MERGED[:, n, :], in0=a[:, 0:T], in1=sg[:, 0:T], op=ALU.add),
                             reads=[sgr, self.MBr], writes=[MERGEDr])

        self.gemm(blocks, compute)


def fm(v):
    v = np.asarray(v)
    lead = v.shape[:-1]
    c = v.shape[-1] // 128
    v = v.reshape(lead + (c, 128))
    return np.ascontiguousarray(np.moveaxis(v, -1, 0))


def rep(v):
    v = np.asarray(v).reshape(1, -1)
    return np.ascontiguousarray(np.broadcast_to(v, (128, v.shape[1])))


def mixer_consts():
    k = np.arange(128)
    ident = np.eye(128, dtype=np.float32)
    ones = np.ones((128, 128), np.float32)
    tri = (k[:, None] <= k[None, :]).astype(np.float32)
    trit = (k[:, None] >= k[None, :]).astype(np.float32)
    negm = np.where(k[:, None] > k[None, :], -30000.0, 0.0).astype(np.float32)
    negmt = np.where(k[:, None] < k[None, :], -30000.0, 0.0).astype(np.float32)
    sel127 = np.zeros((128, 128), np.float32); sel127[127, :] = 1
    sel0 = np.zeros((128, 128), np.float32); sel0[0, :] = 1
    return np.ascontiguousarray(np.stack([ident, ones, tri, trit, negm, negmt, sel127, sel0], axis=1))


class MixBase:
    def __init__(self, c, es, consts_ap, nps=8):
        self.c = c
        nc = self.nc = c.nc
        self.es = es
        self.sb = lambda name, shape, dt: es.enter_context(nc.sbuf_tensor(name, shape, dt))
        self.nps = nps
        self.ps = [es.enter_context(nc.psum_tensor("mps%d" % i, [128, 512], F32)) for i in range(nps)]
        self.psr = [Res("mps%d" % i, excl=True) for i in range(nps)]
        self.pspos = 0
        self.K = self.sb("mconst", [128, 8, 128], F32); self.Kr = Res("mconst")
        c.dma("sp", self.K[:], consts_ap, writes=[self.Kr])
        self.Kb = self.sb("mconstb", [128, 8, 128], BF16)
        c.op("dve", lambda: nc.vector.tensor_copy(out=self.Kb[:], in_=self.K[:]), reads=[self.Kr], writes=[self.Kr])
        self.IDENT, self.ONES, self.TRI, self.TRIT, self.NEGM, self.NEGMT, self.SEL127, self.SEL0 = [self.K[:, i, :] for i in range(8)]

    def psum(self):
        i = self.pspos; self.pspos = (i + 1) % self.nps
        return self.ps[i], self.psr[i]


class RgLru(MixBase):
    def __init__(self, c, es, consts_ap, Lmax):
        super().__init__(c, es, consts_ap)
        sb = self.sb
        self.Lmax = Lmax
        self.cw = sb("rg_cw", [128, 4, 8], F32); self.cb = sb("rg_cb", [128, 8], F32)
        self.gb = sb("rg_gb", [128, 4, 8], F32); self.lam = sb("rg_lam", [128, 2, 8], F32)
        self.gw = sb("rg_gw", [128, 32, 128], BF16)
        self.pr = Res("rg_prm")
        names = ["X", "xa", "r", "i", "tmp", "hf", "hb", "y", "t"]
        self.buf = {n: sb("rg_" + n, [128, Lmax + 4], F32) for n in names}
        self.bres = {n: Res("rg_" + n) for n in names}
        self.xab = sb("rg_xab", [128, Lmax], BF16); self.xabr = Res()
        self.ob = sb("rg_ob", [128, Lmax], BF16); self.obr = Res()
        self.h0 = sb("rg_h0", [128, 2], F32); self.h0r = Res()
        self.hl = sb("rg_hl", [128, 2], F32); self.hlr = Res()

    def load_params(self, prm):
        c, nc = self.c, self.nc
        for t, k in [(self.cw, "cw"), (self.cb, "cb"), (self.gb, "gb"), (self.lam, "lam")]:
            c.dma("sp", t[:], prm[k], writes=[self.pr])
        c.dma("pool", self.gw[:], prm["gw"].rearrange("a b n c d -> c (a b n) d"), writes=[self.pr])
        c.op("act", lambda: nc.scalar.activation(out=self.lam[:], in_=self.lam[:], func=AF.Exp, scale=-1.0), reads=[self.pr], writes=[self.pr])
        c.op("act", lambda: nc.scalar.activation(out=self.lam[:], in_=self.lam[:], func=AF.Ln, bias=1.0), reads=[self.pr], writes=[self.pr])
        c.op("dve", lambda: nc.vector.tensor_scalar(out=self.lam[:], in0=self.lam[:], scalar1=-8.0, scalar2=None, op0=ALU.mult),
             reads=[self.pr], writes=[self.pr])

    def run_seq(self, FMX, BRD, tok0, L, h0_ap=None, hl_out=None):
        c, nc = self.c, self.nc
        B, R = self.buf, self.bres
        for n in range(8):
            X = B["X"]
            c.op("dve", lambda: nc.vector.memset(X[:, 0:2], 0.0), writes=[R["X"]])
            c.op("dve", lambda: nc.vector.memset(X[:, L + 2:L + 3], 0.0), writes=[R["X"]])
            c.dma("sp", X[:, 2:L + 2], FMX.ap[n, :, tok0:tok0 + L], reads=[FMX.res[n]], writes=[R["X"]])
            c.dma("sp", B["y"][:, 0:L], FMX.ap[8 + n, :, tok0:tok0 + L], reads=[FMX.res[8 + n]], writes=[R["y"]])
            xa = B["xa"]
            c.op("dve", lambda: nc.vector.tensor_scalar(out=xa[:, 0:L], in0=X[:, 0:L], scalar1=self.cw[:, 0, n:n + 1], scalar2=self.cb[:, n:n + 1],
                                                      op0=ALU.mult, op1=ALU.add), reads=[R["X"], self.pr], writes=[R["xa"]])
            for j in range(1, 4):
                c.op("dve", lambda j=j: nc.vector.scalar_tensor_tensor(out=xa[:, 0:L], in0=X[:, j:j + L], scalar=self.cw[:, j, n:n + 1], in1=xa[:, 0:L],
                                                                     op0=ALU.mult, op1=ALU.add), reads=[R["X"], R["xa"], self.pr], writes=[R["xa"]])
            c.op("act", lambda: nc.scalar.copy(out=self.xab[:, 0:L], in_=xa[:, 0:L]), reads=[R["xa"]], writes=[self.xabr])
            y, t = B["y"], B["t"]
            c.op("dve", lambda: nc.vector.tensor_tensor(out=t[:, 0:L], in0=y[:, 0:L], in1=y[:, 0:L], op=ALU.mult), reads=[R["y"]], writes=[R["t"]])
            c.op("dve", lambda: nc.vector.tensor_scalar(out=t[:, 0:L], in0=t[:, 0:L], scalar1=0.044715, scalar2=1.0, op0=ALU.mult, op1=ALU.add),
                 reads=[R["t"]], writes=[R["t"]])
            c.op("dve", lambda: nc.vector.tensor_tensor(out=t[:, 0:L], in0=t[:, 0:L], in1=y[:, 0:L], op=ALU.mult), reads=[R["t"], R["y"]], writes=[R["t"]])
            c.op("act", lambda: nc.scalar.activation(out=t[:, 0:L], in_=t[:, 0:L], func=AF.Sigmoid, scale=1.5957691216057308), reads=[R["t"]], writes=[R["t"]])
            c.op("dve", lambda: nc.vector.tensor_tensor(out=y[:, 0:L], in0=y[:, 0:L], in1=t[:, 0:L], op=ALU.mult), reads=[R["t"], R["y"]], writes=[R["y"]])
            if h0_ap is not None:
                for d in range(2):
                    c.dma("sp", self.h0[:, d:d + 1], h0_ap[d, n, :].rearrange("(p a) -> p a", a=1), writes=[self.h0r])
            for d in range(2):
                for g, nm in ((0, "r"), (1, "i")):
                    for t0 in range(0, L, 512):
                        w = min(512, L - t0)
                        ps, psr = self.psum()
                        c.op("pe", lambda: nc.tensor.matmul(ps[:, 0:w], lhsT=self.gw[:, (d * 2 + g) * 8 + n, :], rhs=self.xab[:, t0:t0 + w], start=True, stop=True),
                             reads=[self.pr, self.xabr], writes=[psr])
                        c.op("act", lambda: nc.scalar.activation(out=B[nm][:, t0:t0 + w], in_=ps[:, 0:w], func=AF.Sigmoid, bias=self.gb[:, d * 2 + g, n:n + 1]),
                             reads=[psr, self.pr], writes=[R[nm]])
                r, i, tmp = B["r"], B["i"], B["tmp"]
                c.op("act", lambda: nc.scalar.activation(out=r[:, 0:L], in_=r[:, 0:L], func=AF.Exp, scale=self.lam[:, d, n:n + 1]), reads=[R["r"], self.pr], writes=[R["r"]])
                c.op("dve", lambda: nc.vector.tensor_tensor(out=tmp[:, 0:L], in0=r[:, 0:L], in1=r[:, 0:L], op=ALU.mult), reads=[R["r"]], writes=[R["tmp"]])
                c.op("act", lambda: nc.scalar.activation(out=tmp[:, 0:L], in_=tmp[:, 0:L], func=AF.Sqrt, scale=-1.0, bias=1.0), reads=[R["tmp"]], writes=[R["tmp"]])
                c.op("dve", lambda: nc.vector.tensor_tensor(out=i[:, 0:L], in0=i[:, 0:L], in1=tmp[:, 0:L], op=ALU.mult), reads=[R["i"], R["tmp"]], writes=[R["i"]])
                c.op("dve", lambda: nc.vector.tensor_tensor(out=i[:, 0:L], in0=i[:, 0:L], in1=xa[:, 0:L], op=ALU.mult), reads=[R["i"], R["xa"]], writes=[R["i"]])
                hn = "hf" if d == 0 else "hb"
                h = B[hn]
                init = self.h0[:, d:d + 1] if h0_ap is not None else 0.0
                if d == 0:
                    c.op("dve", lambda: nc.vector.tensor_tensor_scan(out=h[:, 0:L], data0=r[:, 0:L], data1=i[:, 0:L], initial=init, op0=ALU.mult, op1=ALU.add),
                         reads=[R["r"], R["i"], self.h0r], writes=[R[hn]])
                else:
                    c.op("dve", lambda: nc.vector.tensor_tensor_scan(out=h[:, L - 1::-1] if False else h[:, 0:L][:, ::-1], data0=r[:, 0:L][:, ::-1], data1=i[:, 0:L][:, ::-1],
                                                                  initial=init, op0=ALU.mult, op1=ALU.add), reads=[R["r"], R["i"], self.h0r], writes=[R[hn]])
            hf, hb = B["hf"], B["hb"]
            if hl_out is not None:
                c.op("act", lambda: nc.scalar.copy(out=self.hl[:, 0:1], in_=hf[:, L - 1:L]), reads=[R["hf"]], writes=[self.hlr])
                c.op("act", lambda: nc.scalar.copy(out=self.hl[:, 1:2], in_=hb[:, 0:1]), reads=[R["hb"]], writes=[self.hlr])
                for d in range(2):
                    c.dma("act", hl_out[d, n, :].rearrange("(p a) -> p a", a=1), self.hl[:, d:d + 1], reads=[self.hlr])
            c.op("dve", lambda: nc.vector.tensor_tensor(out=hf[:, 0:L], in0=hf[:, 0:L], in1=hb[:, 0:L], op=ALU.add), reads=[R["hf"], R["hb"]], writes=[R["hf"]])
            c.op("dve", lambda: nc.vector.tensor_tensor(out=self.ob[:, 0:L], in0=hf[:, 0:L], in1=y[:, 0:L], op=ALU.mult), reads=[R["hf"], R["y"]], writes=[self.obr])
            c.dma("act", BRD.ap[n, :, tok0:tok0 + L], self.ob[:, 0:L], reads=[self.obr], writes=[BRD.res[n]])


def bc(ap, shape):
    return ap.broadcast_to(list(shape))


class Ssd(MixBase):
    def __init__(self, c, es, consts_ap, Lmax):
        super().__init__(c, es, consts_ap, nps=7)
        sb, nc = self.sb, self.nc
        self.Lmax = Lmax
        nch = self.nchmax = Lmax // 128
        self.pbf = es.enter_context(nc.psum_tensor("mpsbf", [128, 1024], BF16)); self.pbfr = Res("pbf", excl=True)
        self.cw = sb("sd_cw", [128, 4, 12], F32); self.cb = sb("sd_cb", [128, 12], F32)
        self.dtb = sb("sd_dtb", [128, 32], F32); self.A = sb("sd_A", [128, 32], F32); self.Dr = sb("sd_D", [128, 16], F32)
        self.nw = sb("sd_nw", [128, 8], F32)
        self.Dfull = sb("sd_Dfull", [128, 512], F32); self.Dfr = Res()
        self.pr = Res("sd_prm")
        self.X = sb("sd_X", [128, Lmax + 4], F32); self.Xr = Res()
        self.xc = sb("sd_xc", [128, Lmax], F32); self.xcr = Res()
        self.xtm = sb("sd_xtm", [128, nch, 512], BF16); self.xtmr = Res()
        self.Y2 = sb("sd_Y", [128, 2, nch, 512], F32); self.Yr = Res()
        self.SSQ = sb("sd_SSQ", [128, 2, nch], F32); self.SSQr = Res()
        self.Btm = sb("sd_Btm", [128, nch, 128], BF16); self.Btmr = Res()
        self.Bfm = sb("sd_Bfm", [128, Lmax], BF16); self.Cfm = sb("sd_Cfm", [128, Lmax], BF16); self.BCr = Res()
        self.dt = sb("sd_dt", [128, nch, 32], F32); self.dtA = sb("sd_dtA", [128, nch, 32], F32); self.dtr = Res()
        self.S = sb("sd_S", [128, 8, 64], F32); self.Sb = sb("sd_Sb", [128, 8, 64], BF16); self.Sr = Res(); self.Sbr = Res()
        self.CBT = sb("sd_CBT", [128, 128], F32); self.CBTr = Res()
        self.dtAbc = sb("sd_dtAbc", [128, 8, 128], F32); self.dtAbcr = Res()
        self.E = [sb("sd_E%d" % i, [128, 4, 128], F32) for i in range(2)]; self.Er = [Res(), Res()]
        self.M = sb("sd_M", [128, 8, 128], BF16); self.Mr = Res()
        self.xdt = sb("sd_xdt", [128, 8, 64], BF16); self.xw = sb("sd_xw", [128, 8, 64], BF16); self.xdtr = Res(); self.xwr = Res()
        self.tmp = sb("sd_tmp", [128, 512], F32); self.tmpr = Res()
        self.sm = sb("sd_sm", [128, 6, 8], F32); self.smr = Res()
        self.z = sb("sd_z", [128, 512], F32); self.zr = Res()
        self.g = sb("sd_g", [128, 512], F32); self.gr = Res()
        self.gb = sb("sd_gb", [128, 512], BF16); self.gbr = Res()
        self.ss = sb("sd_ss", [128, 2], F32); self.ssr = Res()
        self.ob = sb("sd_ob", [128, 8, 512], BF16); self.obr = Res()

    def load_params(self, prm):
        c, nc = self.c, self.nc
        for t, k in [(self.cw, "cw"), (self.cb, "cb"), (self.dtb, "dtb"), (self.A, "alog"), (self.Dr, "D"), (self.nw, "nw")]:
            c.dma("sp", t[:], prm[k], writes=[self.pr])
        c.op("act", lambda: nc.scalar.activation(out=self.A[:], in_=self.A[:], func=AF.Exp), reads=[self.pr], writes=[self.pr])
        c.op("dve", lambda: nc.vector.tensor_scalar(out=self.A[:], in0=self.A[:], scalar1=-1.0, scalar2=None, op0=ALU.mult), reads=[self.pr], writes=[self.pr])

    def conv_chunk(self, FMX, ch, ci, tok0, L):
        c, nc = self.c, self.nc
        X, xc = self.X, self.xc
        c.op("dve", lambda: nc.vector.memset(X[:, 0:2], 0.0), writes=[self.Xr])
        c.op("dve", lambda: nc.vector.memset(X[:, L + 2:L + 3], 0.0), writes=[self.Xr])
        c.dma("sp", X[:, 2:L + 2], FMX.ap[ch, :, tok0:tok0 + L], reads=[FMX.res[ch]], writes=[self.Xr])
        c.op("dve", lambda: nc.vector.tensor_scalar(out=xc[:, 0:L], in0=X[:, 0:L], scalar1=self.cw[:, 0, ci:ci + 1], scalar2=self.cb[:, ci:ci + 1],
                                                  op0=ALU.mult, op1=ALU.add), reads=[self.Xr, self.pr], writes=[self.xcr])
        for j in range(1, 4):
            c.op("dve", lambda j=j: nc.vector.scalar_tensor_tensor(out=xc[:, 0:L], in0=X[:, j:j + L], scalar=self.cw[:, j, ci:ci + 1], in1=xc[:, 0:L],
                                                                 op0=ALU.mult, op1=ALU.add), reads=[self.Xr, self.xcr, self.pr], writes=[self.xcr])
        c.op("act", lambda: nc.scalar.activation(out=xc[:, 0:L], in_=xc[:, 0:L], func=AF.Silu), reads=[self.xcr], writes=[self.xcr])

    def run_seq(self, FMX, cx0, ZTM, DTIF, BRD, br0, tok0, L, S0T=None, Sout=None):
        c, nc = self.c, self.nc
        nch = L // 128
        c.dma("sp", self.dt[:, 0:nch, :], DTIF.ap[tok0:tok0 + L, 0:32].rearrange("(c p) k -> p c k", p=128), reads=[DTIF.res[0]], writes=[self.dtr])
        dtv = self.dt[:, 0:nch, :]
        c.op("dve", lambda: nc.vector.tensor_tensor(out=dtv, in0=dtv, in1=bc(self.dtb[:, :].unsqueeze(1), [128, nch, 32]), op=ALU.add), reads=[self.dtr, self.pr], writes=[self.dtr])
        c.op("act", lambda: nc.scalar.activation(out=dtv, in_=dtv, func=AF.Exp), reads=[self.dtr], writes=[self.dtr])
        c.op("act", lambda: nc.scalar.activation(out=dtv, in_=dtv, func=AF.Ln, bias=1.0), reads=[self.dtr], writes=[self.dtr])
        c.op("dve", lambda: nc.vector.tensor_tensor(out=self.dtA[:, 0:nch, :], in0=dtv, in1=bc(self.A[:, :].unsqueeze(1), [128, nch, 32]), op=ALU.mult),
             reads=[self.dtr, self.pr], writes=[self.dtr])
        import os
        self.stop = int(os.environ.get("SSD_STOP", "99"))
        if self.stop <= 1:
            return
        for g in range(2):
            self.run_group(FMX, cx0, ZTM, BRD, br0, tok0, L, g, S0T, Sout)
        if self.stop <= 6:
            return
        for t4 in range(0, nch, 4):
            n4 = min(4, nch - t4)
            for k in range(n4):
                ct = t4 + k
                ss = self.ss
                c.op("dve", lambda: nc.vector.tensor_tensor(out=ss[:, 0:1], in0=self.SSQ[:, 0, ct:ct + 1], in1=self.SSQ[:, 1, ct:ct + 1], op=ALU.add), reads=[self.SSQr], writes=[self.ssr])
                c.op("dve", lambda: nc.vector.tensor_scalar(out=ss[:, 0:1], in0=ss[:, 0:1], scalar1=1.0 / 1024, scalar2=1e-6, op0=ALU.mult, op1=ALU.add), reads=[self.ssr], writes=[self.ssr])
                c.op("act", lambda: nc.scalar.activation(out=ss[:, 0:1], in_=ss[:, 0:1], func=AF.Sqrt), reads=[self.ssr], writes=[self.ssr])
                c.op("dve", lambda: nc.vector.reciprocal(out=ss[:, 1:2], in_=ss[:, 0:1]), reads=[self.ssr], writes=[self.ssr])
                for g in range(2):
                    c.op("dve", lambda: nc.vector.tensor_scalar(out=self.gb[:], in0=self.Y2[:, g, ct, :], scalar1=ss[:, 1:2], scalar2=None, op0=ALU.mult),
                         reads=[self.Yr, self.ssr], writes=[self.gbr])
                    for q in range(4):
                        c.op("pe", lambda q=q: nc.tensor.transpose(out=self.pbf[:, q * 128:(q + 1) * 128], in_=self.gb[:, q * 128:(q + 1) * 128], identity=self.Kb[:, 0, :]),
                             reads=[self.gbr, self.Kr], writes=[self.pbfr])
                    for q in range(4):
                        ci = 4 * g + q
                        c.op("act", lambda q=q, ci=ci: nc.scalar.activation(out=self.ob[:, ci, k * 128:(k + 1) * 128], in_=self.pbf[:, q * 128:(q + 1) * 128], func=AF.Copy,
                                                                            scale=self.nw[:, ci:ci + 1]), reads=[self.pbfr, self.pr], writes=[self.obr])
            c.dma("act", BRD.ap[br0:br0 + 8, :, tok0 + t4 * 128: tok0 + (t4 + n4) * 128].rearrange("c p t -> p c t"), self.ob[:, :, 0:n4 * 128], reads=[self.obr],
                  writes=[BRD.res[br0 + i] for i in range(8)])

    def run_group(self, FMX, cx0, ZTM, BRD, br0, tok0, L, g, S0T, Sout):
        c, nc = self.c, self.nc
        nch = L // 128
        self.Y = self.Y2[:, g, :, :]
        c.op("dve", lambda: nc.vector.tensor_tensor(out=self.Dfull[:].rearrange("p (h e) -> p h e", h=8), in0=bc(self.ONES[:, 0:64].unsqueeze(1), [128, 8, 64]),
                                                  in1=bc(self.Dr[:, 8 * g:8 * g + 8].unsqueeze(2), [128, 8, 64]), op=ALU.mult), reads=[self.Kr, self.pr], writes=[self.Dfr])
        for q in range(4):
            ci = 4 * g + q
            import os
            sub = int(os.environ.get("SSD_SUB", "99"))
            self.conv_chunk(FMX, cx0 + ci, ci, tok0, L)
            if sub <= 0:
                continue
            for t4 in range(0, nch, 4):
                n4 = min(4, nch - t4)
                ps, psr = self.psum()
                for k in range(n4):
                    c.op("pe", lambda k=k: nc.tensor.transpose(out=ps[:, k * 128:(k + 1) * 128], in_=self.xc[:, (t4 + k) * 128:(t4 + k + 1) * 128], identity=self.IDENT),
                         reads=[self.xcr, self.Kr], writes=[psr])
                pv = ps[:, 0:n4 * 128].rearrange("p (k h e) -> p k h e", k=n4, h=2)
                if sub <= 1:
                    continue
                c.op("act", lambda: nc.scalar.copy(out=self.xtm[:, t4:t4 + n4, q * 128:(q + 1) * 128], in_=ps[:, 0:n4 * 128].rearrange("p (k e) -> p k e", k=n4)),
                     reads=[psr], writes=[self.xtmr])
                if sub <= 2:
                    continue
                for k in range(n4):
                    c.op("dve", lambda k=k: nc.vector.tensor_tensor(out=self.Y[:, t4 + k, q * 128:(q + 1) * 128], in0=ps[:, k * 128:(k + 1) * 128],
                                                                  in1=self.Dfull[:, q * 128:(q + 1) * 128], op=ALU.mult), reads=[psr, self.Dfr], writes=[self.Yr])
        if self.stop <= 2:
            return
        self.conv_chunk(FMX, cx0 + 8 + g, 8 + g, tok0, L)
        c.op("act", lambda: nc.scalar.copy(out=self.Bfm[:, 0:L], in_=self.xc[:, 0:L]), reads=[self.xcr], writes=[self.BCr])
        for t8 in range(0, nch, 8):
            n8 = min(8, nch - t8)
            for k in range(n8):
                c.op("pe", lambda k=k: nc.tensor.transpose(out=self.pbf[:, k * 128:(k + 1) * 128], in_=self.Bfm[:, (t8 + k) * 128:(t8 + k + 1) * 128], identity=self.Kb[:, 0, :]),
                     reads=[self.BCr, self.Kr], writes=[self.pbfr])
            c.op("act", lambda: nc.scalar.copy(out=self.Btm[:, t8:t8 + n8, :], in_=self.pbf[:, 0:n8 * 128].rearrange("p (k e) -> p k e", k=n8)),
                 reads=[self.pbfr], writes=[self.Btmr])
        self.conv_chunk(FMX, cx0 + 10 + g, 10 + g, tok0, L)
        c.op("act", lambda: nc.scalar.copy(out=self.Cfm[:, 0:L], in_=self.xc[:, 0:L]), reads=[self.xcr], writes=[self.BCr])
        if self.stop <= 3:
            return
        for d in range(2):
            TRI = self.TRI if d == 0 else self.TRIT
            NEG = self.NEGM if d == 0 else self.NEGMT
            hs = d * 16 + 8 * g
            if S0T is not None:
                c.dma("sp", self.S[:], S0T[d, :, 8 * g:8 * g + 8, :], writes=[self.Sr])
            else:
                c.op("dve", lambda: nc.vector.memset(self.S[:], 0.0), writes=[self.Sr])
            c.op("act", lambda: nc.scalar.copy(out=self.Sb[:], in_=self.S[:]), reads=[self.Sr], writes=[self.Sbr])
            for ct in (range(nch) if d == 0 else range(nch - 1, -1, -1)):
                t0 = ct * 128
                dA = self.dtA[:, ct, hs:hs + 8]
                sm = self.sm
                ps, psr = self.psum()
                c.op("pe", lambda: nc.tensor.matmul(ps[:, 0:128], lhsT=self.Bfm[:, t0:t0 + 128], rhs=self.Cfm[:, t0:t0 + 128], start=True, stop=True),
                     reads=[self.BCr], writes=[psr])
                c.op("act", lambda: nc.scalar.copy(out=self.CBT[:], in_=ps[:, 0:128]), reads=[psr], writes=[self.CBTr])
                p2, p2r = self.psum()
                c.op("pe", lambda: nc.tensor.matmul(p2[:, 0:8], lhsT=TRI, rhs=dA, start=True, stop=True), reads=[self.Kr, self.dtr], writes=[p2r])
                c.op("pe", lambda: nc.tensor.matmul(p2[:, 8:16], lhsT=self.ONES, rhs=dA, start=True, stop=True), reads=[self.Kr, self.dtr], writes=[p2r])
                c.op("dve", lambda: nc.vector.tensor_copy(out=sm[:, 0:2, :], in_=p2[:, 0:16].rearrange("p (a b) -> p a b", a=2)), reads=[p2r], writes=[self.smr])
                c.op("act", lambda: nc.scalar.activation(out=sm[:, 2, :], in_=sm[:, 0, :], func=AF.Exp), reads=[self.smr], writes=[self.smr])
                c.op("dve", lambda: nc.vector.tensor_scalar(out=sm[:, 3, :], in0=sm[:, 0, :], scalar1=-1.0, scalar2=None, op0=ALU.mult), reads=[self.smr], writes=[self.smr])
                c.op("dve", lambda: nc.vector.tensor_tensor(out=sm[:, 4, :], in0=sm[:, 1, :], in1=sm[:, 0, :], op=ALU.subtract), reads=[self.smr], writes=[self.smr])
                c.op("act", lambda: nc.scalar.activation(out=sm[:, 4, :], in_=sm[:, 4, :], func=AF.Exp), reads=[self.smr], writes=[self.smr])
                c.op("act", lambda: nc.scalar.activation(out=sm[:, 5, :], in_=sm[:, 1, :], func=AF.Exp), reads=[self.smr], writes=[self.smr])
                c.op("dve", lambda: nc.vector.tensor_tensor(out=self.dtAbc[:], in0=bc(self.ONES.unsqueeze(1), [128, 8, 128]), in1=bc(dA.unsqueeze(2), [128, 8, 128]), op=ALU.mult),
                     reads=[self.Kr, self.dtr], writes=[self.dtAbcr])
                for hh in range(2):
                    pd, pdr = self.psum()
                    for k in range(4):
                        h = hh * 4 + k
                        c.op("pe", lambda k=k, h=h: nc.tensor.matmul(pd[:, k * 128:(k + 1) * 128], lhsT=self.dtAbc[:, h, :], rhs=TRI, start=True, stop=False),
                             reads=[self.dtAbcr, self.Kr], writes=[pdr])
                        c.op("pe", lambda k=k: nc.tensor.matmul(pd[:, k * 128:(k + 1) * 128], lhsT=self.IDENT, rhs=NEG, start=False, stop=True),
                             reads=[self.Kr], writes=[pdr])
                    E, Er = self.E[hh], self.Er[hh]
                    c.op("dve", lambda: nc.vector.tensor_tensor(out=E[:], in0=pd[:, 0:512].rearrange("p (k t) -> p k t", k=4),
                                                              in1=bc(sm[:, 3, hh * 4:hh * 4 + 4].unsqueeze(2), [128, 4, 128]), op=ALU.add), reads=[pdr, self.smr], writes=[Er])
                    c.op("act", lambda: nc.scalar.activation(out=E[:], in_=E[:], func=AF.Exp), reads=[Er], writes=[Er])
                    c.op("dve", lambda: nc.vector.tensor_tensor(out=self.M[:, hh * 4:hh * 4 + 4, :], in0=E[:], in1=bc(self.CBT[:, :].unsqueeze(1), [128, 4, 128]), op=ALU.mult),
                         reads=[Er, self.CBTr], writes=[self.Mr])
                xv = self.xtm[:, ct, :].rearrange("p (h e) -> p h e", h=8)
                c.op("dve", lambda: nc.vector.tensor_tensor(out=self.xdt[:], in0=xv, in1=bc(self.dt[:, ct, hs:hs + 8].unsqueeze(2), [128, 8, 64]), op=ALU.mult),
                     reads=[self.xtmr, self.dtr], writes=[self.xdtr])
                c.op("dve", lambda: nc.vector.tensor_tensor(out=self.xw[:], in0=self.xdt[:], in1=bc(sm[:, 4, :].unsqueeze(2), [128, 8, 64]), op=ALU.mult),
                     reads=[self.xdtr, self.smr], writes=[self.xwr])
                py1, py1r = self.psum()
                for h in range(8):
                    c.op("pe", lambda h=h: nc.tensor.matmul(py1[:, h * 64:(h + 1) * 64], lhsT=self.M[:, h, :], rhs=self.xdt[:, h, :], start=True, stop=True),
                         reads=[self.Mr, self.xdtr], writes=[py1r])
                py2, py2r = self.psum()
                c.op("pe", lambda: nc.tensor.matmul(py2[:, 0:512], lhsT=self.Cfm[:, t0:t0 + 128], rhs=self.Sb[:].rearrange("p h e -> p (h e)"), start=True, stop=True),
                     reads=[self.BCr, self.Sbr], writes=[py2r])
                pds, pdsr = self.psum()
                c.op("pe", lambda: nc.tensor.matmul(pds[:, 0:512], lhsT=self.Btm[:, ct, :], rhs=self.xw[:].rearrange("p h e -> p (h e)"), start=True, stop=True),
                     reads=[self.Btmr, self.xwr], writes=[pdsr])
                c.op("dve", lambda: nc.vector.tensor_tensor(out=self.tmp[:].rearrange("p (h e) -> p h e", h=8), in0=py2[:, 0:512].rearrange("p (h e) -> p h e", h=8),
                                                          in1=bc(sm[:, 2, :].unsqueeze(2), [128, 8, 64]), op=ALU.mult), reads=[py2r, self.smr], writes=[self.tmpr])
                c.op("dve", lambda: nc.vector.tensor_tensor(out=self.tmp[:], in0=self.tmp[:], in1=py1[:, 0:512], op=ALU.add), reads=[py1r, self.tmpr], writes=[self.tmpr])
                c.op("pool", lambda: nc.gpsimd.tensor_tensor(out=self.Y[:, ct, :], in0=self.Y[:, ct, :], in1=self.tmp[:], op=ALU.add), reads=[self.tmpr, self.Yr], writes=[self.Yr])
                c.op("dve", lambda: nc.vector.tensor_tensor(out=self.S[:], in0=self.S[:], in1=bc(sm[:, 5, :].unsqueeze(2), [128, 8, 64]), op=ALU.mult),
                     reads=[self.Sr, self.smr], writes=[self.Sr])
                c.op("dve", lambda: nc.vector.tensor_tensor(out=self.S[:].rearrange("p h e -> p (h e)"), in0=self.S[:].rearrange("p h e -> p (h e)"), in1=pds[:, 0:512], op=ALU.add),
                     reads=[self.Sr, pdsr], writes=[self.Sr])
                c.op("act", lambda: nc.scalar.copy(out=self.Sb[:], in_=self.S[:]), reads=[self.Sr], writes=[self.Sbr])
            if Sout is not None:
                c.dma("act", Sout[d, :, 8 * g:8 * g + 8, :], self.S[:], reads=[self.Sr])
        if self.stop <= 5:
            return
        for ct in range(nch):
            c.dma("sp", self.z[:], ZTM.ap[tok0 + ct * 128: tok0 + (ct + 1) * 128, 512 * g:512 * (g + 1)], reads=[ZTM.res[0]], writes=[self.zr])
            c.op("act", lambda: nc.scalar.activation(out=self.z[:], in_=self.z[:], func=AF.Silu), reads=[self.zr], writes=[self.zr])
            c.op("dve", lambda: nc.vector.tensor_tensor(out=self.Y[:, ct, :], in0=self.Y[:, ct, :], in1=self.z[:], op=ALU.mult), reads=[self.zr, self.Yr], writes=[self.Yr])
            c.op("act", lambda: nc.scalar.activation(out=self.g[:], in_=self.Y[:, ct, :], func=AF.Square, accum_out=self.SSQ[:, g, ct:ct + 1]), reads=[self.Yr], writes=[self.gr, self.SSQr])


class Mlstm(MixBase):
    def __init__(self, c, es, consts_ap, Lmax, rope):
        super().__init__(c, es, consts_ap, nps=7)
        sb, nc = self.sb, self.nc
        nch = Lmax // 128
        self.pbf = es.enter_context(nc.psum_tensor("mpsbf", [128, 1024], BF16)); self.pbfr = Res("pbf", excl=True)
        self.gbias = sb("ml_gbias", [128, 16], F32); self.nw = sb("ml_nw", [128, 8], F32); self.pr = Res()
        self.qkv = sb("ml_qkv", [128, nch, 3, 256], F32); self.qkvr = Res()
        self.ktm = sb("ml_ktm", [128, nch, 256], BF16); self.v1 = sb("ml_v1", [128, nch, 257], BF16); self.qb = sb("ml_qb", [128, nch, 256], BF16)
        self.cvr = Res()
        self.qT = sb("ml_qT", [128, 2, Lmax], BF16); self.kT = sb("ml_kT", [128, 2, Lmax], BF16); self.qkTr = Res()
        self.H = sb("ml_H", [128, nch, 256], F32); self.Hr = Res()
        self.sigo = sb("ml_sigo", [128, 2, Lmax], F32); self.sigor = Res()
        self.ob = sb("ml_ob", [128, 2, Lmax], BF16); self.obr = Res()
        self.rope = rope
        if rope:
            self.cos = sb("ml_cos", [128, nch, 128], F32); self.sin = sb("ml_sin", [128, nch, 128], F32); self.csr = Res()
            self.rt = [sb("ml_rt%d" % i, [128, nch, 2, 64], F32) for i in range(4)]; self.rtr = Res()
        self.G = sb("ml_G", [128, nch, 16], F32); self.LSF = sb("ml_LSF", [128, nch, 16], F32); self.Gr = Res()
        self.CnT = sb("ml_CnT", [128, 2, 257], F32); self.CnTb = sb("ml_CnTb", [128, 2, 257], BF16); self.Cr = Res(); self.Cbr = Res()
        self.mcol = sb("ml_mcol", [128, 8], F32); self.mr = Res()
        self.R1 = sb("ml_R1", [128, 128], F32); self.R2 = sb("ml_R2", [128, 128], F32); self.tA = sb("ml_tA", [128, 128], F32); self.Rr = Res()
        self.W = sb("ml_W", [128, 128], F32); self.Wr = Res()
        self.Sb = sb("ml_Sb", [128, 128], BF16); self.Sbr = Res(); self.ST = sb("ml_ST", [128, 128], BF16); self.STr = Res()
        self.st = sb("ml_st", [128, 24], F32); self.str_ = Res()
        self.tmpc = sb("ml_tmpc", [128, 256], F32); self.tmpcr = Res()
        self.hc = sb("ml_hc", [128, 256], F32); self.hcr = Res()
        self.kw = sb("ml_kw", [128, 256], BF16); self.kwr = Res()
        self.hn = sb("ml_hn", [128, 256], BF16); self.hnr = Res()
        self.bs = sb("ml_bs", [128, 8], F32); self.bsr = Res()

    def load_params(self, prm):
        c = self.c
        c.dma("sp", self.gbias[:], prm["gbias"], writes=[self.pr])
        c.dma("sp", self.nw[:], prm["nw"], writes=[self.pr])

    def run_seq(self, QKV, DTIF, FMX, o0, BRD, br0, tok0, L, st_in=None, st_out=None, rope_tabs=None):
        c, nc = self.c, self.nc
        nch = L // 128
        V = lambda ap: ap.rearrange("(c p) e -> p c e", p=128)
        c.dma("sp", self.G[:, 0:nch, :], V(DTIF.ap[tok0:tok0 + L, 32:48]), reads=[DTIF.res[0]], writes=[self.Gr])
        Gv, Lv = self.G[:, 0:nch, :], self.LSF[:, 0:nch, :]
        c.op("dve", lambda: nc.vector.tensor_tensor(out=Gv, in0=Gv, in1=bc(self.gbias[:, :].unsqueeze(1), [128, nch, 16]), op=ALU.add), reads=[self.Gr, self.pr], writes=[self.Gr])
        c.op("act", lambda: nc.scalar.activation(out=Lv, in_=Gv, func=AF.Exp, scale=-1.0), reads=[self.Gr], writes=[self.Gr])
        c.op("act", lambda: nc.scalar.activation(out=Lv, in_=Lv, func=AF.Ln, bias=1.0), reads=[self.Gr], writes=[self.Gr])
        c.op("dve", lambda: nc.vector.tensor_scalar(out=Lv, in0=Lv, scalar1=-1.0, scalar2=None, op0=ALU.mult), reads=[self.Gr], writes=[self.Gr])
        self.use_rope = rope_tabs is not None
        if self.use_rope:
            c.dma("sp", self.cos[:, 0:nch, :], V(rope_tabs[0]), writes=[self.csr])
            c.dma("sp", self.sin[:, 0:nch, :], V(rope_tabs[1]), writes=[self.csr])
        if st_in is not None:
            c.dma("sp", self.mcol[:], st_in["m0rep"], writes=[self.mr])
        for h in range(4):
            self.run_head(QKV, FMX, o0, BRD, br0, tok0, L, h, st_in, st_out)

    def run_head(self, QKV, FMX, o0, BRD, br0, tok0, L, h, st_in, st_out):
        c, nc = self.c, self.nc
        nch = L // 128
        V = lambda ap: ap.rearrange("(c p) e -> p c e", p=128)
        for j in range(3):
            c.dma("sp", self.qkv[:, 0:nch, j, :], V(QKV.ap[tok0:tok0 + L, j * 1024 + h * 256: j * 1024 + (h + 1) * 256]), reads=[QKV.res[0]], writes=[self.qkvr])
        for ec in range(2):
            c.dma("sp", self.sigo[:, ec, 0:L], FMX.ap[o0 + h * 2 + ec, :, tok0:tok0 + L], reads=[FMX.res[o0 + h * 2 + ec]], writes=[self.sigor])
        c.op("act", lambda: nc.scalar.activation(out=self.sigo[:, :, 0:L], in_=self.sigo[:, :, 0:L], func=AF.Sigmoid), reads=[self.sigor], writes=[self.sigor])
        if self.use_rope:
            for j in range(2):
                T4 = self.qkv[:, 0:nch, j, :].rearrange("p c (a b e) -> p c a b e", a=2, b=2)
                U1, U2 = T4[:, :, :, 0, :], T4[:, :, :, 1, :]
                C4 = self.cos[:, 0:nch, :].rearrange("p c (a e) -> p c a e", a=2)
                S4 = self.sin[:, 0:nch, :].rearrange("p c (a e) -> p c a e", a=2)
                t = [x[:, 0:nch, :, :] for x in self.rt]
                rd = [self.qkvr, self.csr, self.rtr]
                c.op("dve", lambda: nc.vector.tensor_tensor(out=t[0], in0=U1, in1=C4, op=ALU.mult), reads=rd, writes=[self.rtr])
                c.op("dve", lambda: nc.vector.tensor_tensor(out=t[1], in0=U2, in1=S4, op=ALU.mult), reads=rd, writes=[self.rtr])
                c.op("dve", lambda: nc.vector.tensor_tensor(out=t[2], in0=U1, in1=S4, op=ALU.mult), reads=rd, writes=[self.rtr])
                c.op("dve", lambda: nc.vector.tensor_tensor(out=t[3], in0=U2, in1=C4, op=ALU.mult), reads=rd, writes=[self.rtr])
                c.op("dve", lambda: nc.vector.tensor_tensor(out=U1, in0=t[0], in1=t[1], op=ALU.subtract), reads=rd, writes=[self.qkvr])
                c.op("dve", lambda: nc.vector.tensor_tensor(out=U2, in0=t[2], in1=t[3], op=ALU.add), reads=rd, writes=[self.qkvr])
        c.op("act", lambda: nc.scalar.copy(out=self.qb[:, 0:nch, :], in_=self.qkv[:, 0:nch, 0, :]), reads=[self.qkvr], writes=[self.cvr])
        c.op("act", lambda: nc.scalar.activation(out=self.ktm[:, 0:nch, :], in_=self.qkv[:, 0:nch, 1, :], func=AF.Copy, scale=0.0625), reads=[self.qkvr], writes=[self.cvr])
        c.op("dve", lambda: nc.vector.tensor_copy(out=self.v1[:, 0:nch, 0:256], in_=self.qkv[:, 0:nch, 2, :]), reads=[self.qkvr], writes=[self.cvr])
        c.op("dve", lambda: nc.vector.memset(self.v1[:, 0:nch, 256:257], 1.0), writes=[self.cvr])
        for src, dst in ((self.qb, self.qT), (self.ktm, self.kT)):
            for c4 in range(0, nch, 4):
                n4 = min(4, nch - c4)
                for k in range(n4):
                    for dc in range(2):
                        c.op("pe", lambda k=k, dc=dc: nc.tensor.transpose(out=self.pbf[:, (dc * 4 + k) * 128:(dc * 4 + k + 1) * 128], in_=src[:, c4 + k, dc * 128:(dc + 1) * 128],
                                                                         identity=self.Kb[:, 0, :]), reads=[self.cvr, self.Kr], writes=[self.pbfr])
                c.op("act", lambda: nc.scalar.copy(out=dst[:, :, c4 * 128:(c4 + n4) * 128], in_=self.pbf[:, :].rearrange("p (d x) -> p d x", d=2)[:, :, 0:n4 * 128]),
                     reads=[self.pbfr], writes=[self.qkTr])
        st = self.st
        for d in range(2):
            TRI = self.TRI if d == 0 else self.TRIT
            NEG = self.NEGMT if d == 0 else self.NEGM
            SEL = self.SEL127 if d == 0 else self.SEL0
            mc = self.mcol[:, d * 4 + h: d * 4 + h + 1]
            if st_in is not None:
                c.dma("sp", self.CnT[:, :, 0:256], st_in["C0T"][d, h].rearrange("(a p) e -> p a e", p=128), writes=[self.Cr])
                for dc in range(2):
                    c.dma("sp", self.CnT[:, dc, 256:257], st_in["n0"][d, h, dc * 128:(dc + 1) * 128].rearrange("(p a) -> p a", a=1), writes=[self.Cr])
            else:
                c.op("dve", lambda: nc.vector.memset(self.CnT[:], 0.0), writes=[self.Cr])
                c.op("dve", lambda: nc.vector.memset(mc, 0.0), writes=[self.mr])
            c.op("act", lambda: nc.scalar.copy(out=self.CnTb[:], in_=self.CnT[:]), reads=[self.Cr], writes=[self.Cbr])
            for ct in (range(nch) if d == 0 else range(nch - 1, -1, -1)):
                t0 = ct * 128
                lsf = self.LSF[:, ct, d * 8 + 4 + h: d * 8 + 4 + h + 1]
                ic = self.G[:, ct, d * 8 + h: d * 8 + h + 1]
                c.op("dve", lambda: nc.vector.tensor_scalar(out=self.tA[:], in0=TRI, scalar1=lsf, scalar2=-1.0, op0=ALU.mult, op1=ALU.mult), reads=[self.Kr, self.Gr], writes=[self.Rr])
                c.op("dve", lambda: nc.vector.scalar_tensor_tensor(out=self.R2[:], in0=self.IDENT, scalar=ic, in1=self.tA[:], op0=ALU.mult, op1=ALU.add), reads=[self.Kr, self.Gr, self.Rr], writes=[self.Rr])
                c.op("dve", lambda: nc.vector.tensor_scalar(out=self.R1[:], in0=self.ONES, scalar1=lsf, scalar2=None, op0=ALU.mult), reads=[self.Kr, self.Gr], writes=[self.Rr])
                pd, pdr = self.psum()
                c.op("pe", lambda: nc.tensor.matmul(pd[:, 0:128], lhsT=TRI, rhs=self.R1[:], start=True, stop=False), reads=[self.Kr, self.Rr], writes=[pdr])
                c.op("pe", lambda: nc.tensor.matmul(pd[:, 0:128], lhsT=self.ONES, rhs=self.R2[:], start=False, stop=False), reads=[self.Kr, self.Rr], writes=[pdr])
                c.op("pe", lambda: nc.tensor.matmul(pd[:, 0:128], lhsT=self.IDENT, rhs=NEG, start=False, stop=True), reads=[self.Kr], writes=[pdr])
                c.op("pe", lambda: nc.tensor.matmul(pd[:, 128:129], lhsT=TRI, rhs=lsf, start=True, stop=True), reads=[self.Kr, self.Gr], writes=[pdr])
                c.op("dve", lambda: nc.vector.reduce_max(out=st[:, 0:1], in_=pd[:, 0:128], axis=AX.X), reads=[pdr], writes=[self.str_])
                c.op("dve", lambda: nc.vector.tensor_copy(out=st[:, 5:6], in_=pd[:, 128:129]), reads=[pdr], writes=[self.str_])
                c.op("dve", lambda: nc.vector.tensor_tensor(out=st[:, 1:2], in0=st[:, 5:6], in1=mc, op=ALU.add), reads=[self.str_, self.mr], writes=[self.str_])
                c.op("dve", lambda: nc.vector.tensor_tensor(out=st[:, 6:7], in0=st[:, 0:1], in1=st[:, 1:2], op=ALU.max), reads=[self.str_], writes=[self.str_])
                c.op("dve", lambda: nc.vector.tensor_scalar(out=st[:, 3:4], in0=st[:, 6:7], scalar1=-1.0, scalar2=None, op0=ALU.mult), reads=[self.str_], writes=[self.str_])
                c.op("act", lambda: nc.scalar.activation(out=self.W[:], in_=pd[:, 0:128], func=AF.Exp, bias=st[:, 3:4]), reads=[pdr, self.str_], writes=[self.Wr])
                c.op("act", lambda: nc.scalar.activation(out=st[:, 2:3], in_=st[:, 1:2], func=AF.Exp, bias=st[:, 3:4]), reads=[self.str_], writes=[self.str_])
                c.op("act", lambda: nc.scalar.activation(out=st[:, 7:8], in_=st[:, 3:4], func=AF.Exp), reads=[self.str_], writes=[self.str_])
                pq, pqr = self.psum()
                for dc in range(2):
                    c.op("pe", lambda dc=dc: nc.tensor.matmul(pq[:, 0:128], lhsT=self.qT[:, dc, t0:t0 + 128], rhs=self.kT[:, dc, t0:t0 + 128], start=(dc == 0), stop=(dc == 1)),
                         reads=[self.qkTr], writes=[pqr])
                c.op("dve", lambda: nc.vector.tensor_tensor(out=self.Sb[:], in0=pq[:, 0:128], in1=self.W[:], op=ALU.mult), reads=[pqr, self.Wr], writes=[self.Sbr])
                c.op("pe", lambda: nc.tensor.transpose(out=self.pbf[:, 0:128], in_=self.Sb[:], identity=self.Kb[:, 0, :]), reads=[self.Sbr, self.Kr], writes=[self.pbfr])
                c.op("act", lambda: nc.scalar.copy(out=self.ST[:], in_=self.pbf[:, 0:128]), reads=[self.pbfr], writes=[self.STr])
                pn, pnr = self.psum()
                c.op("pe", lambda: nc.tensor.matmul(pn[:, 0:257], lhsT=self.ST[:], rhs=self.v1[:, ct, :], start=True, stop=True), reads=[self.STr, self.cvr], writes=[pnr])
                pc, pcr = self.psum()
                for dc in range(2):
                    c.op("pe", lambda dc=dc: nc.tensor.matmul(pc[:, 0:257], lhsT=self.qT[:, dc, t0:t0 + 128], rhs=self.CnTb[:, dc, :], start=(dc == 0), stop=(dc == 1)),
                         reads=[self.qkTr, self.Cbr], writes=[pcr])
                c.op("dve", lambda: nc.vector.tensor_tensor(out=st[:, 4:5], in0=pc[:, 256:257], in1=st[:, 2:3], op=ALU.mult), reads=[pcr, self.str_], writes=[self.str_])
                c.op("dve", lambda: nc.vector.tensor_tensor(out=st[:, 4:5], in0=st[:, 4:5], in1=pn[:, 256:257], op=ALU.add), reads=[pnr, self.str_], writes=[self.str_])
                c.op("dve", lambda: nc.vector.tensor_scalar(out=st[:, 17:18], in0=st[:, 4:5], scalar1=-1.0, scalar2=None, op0=ALU.mult), reads=[self.str_], writes=[self.str_])
                c.op("dve", lambda: nc.vector.tensor_tensor(out=st[:, 4:5], in0=st[:, 4:5], in1=st[:, 17:18], op=ALU.max), reads=[self.str_], writes=[self.str_])
                c.op("dve", lambda: nc.vector.tensor_tensor(out=st[:, 4:5], in0=st[:, 4:5], in1=st[:, 7:8], op=ALU.max), reads=[self.str_], writes=[self.str_])
                c.op("dve", lambda: nc.vector.reciprocal(out=st[:, 8:9], in_=st[:, 4:5]), reads=[self.str_], writes=[self.str_])
                c.op("dve", lambda: nc.vector.tensor_tensor(out=st[:, 9:10], in0=st[:, 8:9], in1=st[:, 2:3], op=ALU.mult), reads=[self.str_], writes=[self.str_])
                c.op("act", lambda: nc.scalar.activation(out=self.tmpc[:], in_=pc[:, 0:256], func=AF.Copy, scale=st[:, 9:10]), reads=[pcr, self.str_], writes=[self.tmpcr])
                if d == 0:
                    c.op("dve", lambda: nc.vector.scalar_tensor_tensor(out=self.H[:, ct, :], in0=pn[:, 0:256], scalar=st[:, 8:9], in1=self.tmpc[:], op0=ALU.mult, op1=ALU.add),
                         reads=[pnr, self.str_, self.tmpcr], writes=[self.Hr])
                else:
                    c.op("dve", lambda: nc.vector.scalar_tensor_tensor(out=self.hc[:], in0=pn[:, 0:256], scalar=st[:, 8:9], in1=self.tmpc[:], op0=ALU.mult, op1=ALU.add),
                         reads=[pnr, self.str_, self.tmpcr], writes=[self.hcr])
                    c.op("pool", lambda: nc.gpsimd.tensor_tensor(out=self.H[:, ct, :], in0=self.H[:, ct, :], in1=self.hc[:], op=ALU.add), reads=[self.hcr, self.Hr], writes=[self.Hr])
                pe_, per = self.psum()
                c.op("pe", lambda: nc.tensor.matmul(pe_[:, 0:2], lhsT=SEL, rhs=st[:, 5:7], start=True, stop=True), reads=[self.Kr, self.str_], writes=[per])
                c.op("dve", lambda: nc.vector.tensor_copy(out=st[:, 15:17], in_=pe_[:, 0:2]), reads=[per], writes=[self.str_])
                c.op("dve", lambda: nc.vector.tensor_tensor(out=st[:, 10:11], in0=ic, in1=st[:, 5:6], op=ALU.subtract), reads=[self.Gr, self.str_], writes=[self.str_])
                c.op("dve", lambda: nc.vector.tensor_tensor(out=st[:, 11:12], in0=st[:, 15:16], in1=st[:, 16:17], op=ALU.subtract), reads=[self.str_], writes=[self.str_])
                c.op("act", lambda: nc.scalar.activation(out=st[:, 12:13], in_=st[:, 10:11], func=AF.Exp, bias=st[:, 11:12]), reads=[self.str_], writes=[self.str_])
                c.op("dve", lambda: nc.vector.tensor_tensor(out=st[:, 13:14], in0=st[:, 11:12], in1=mc, op=ALU.add), reads=[self.str_, self.mr], writes=[self.str_])
                c.op("act", lambda: nc.scalar.activation(out=st[:, 14:15], in_=st[:, 13:14], func=AF.Exp), reads=[self.str_], writes=[self.str_])
                c.op("dve", lambda: nc.vector.tensor_scalar(out=self.kw[:], in0=self.ktm[:, ct, :], scalar1=st[:, 12:13], scalar2=None, op0=ALU.mult), reads=[self.cvr, self.str_], writes=[self.kwr])
                for dc in range(2):
                    pk, pkr = self.psum()
                    c.op("pe", lambda dc=dc: nc.tensor.matmul(pk[:, 0:257], lhsT=self.kw[:, dc * 128:(dc + 1) * 128], rhs=self.v1[:, ct, :], start=True, stop=True),
                         reads=[self.kwr, self.cvr], writes=[pkr])
                    c.op("dve", lambda dc=dc: nc.vector.scalar_tensor_tensor(out=self.CnT[:, dc, :], in0=self.CnT[:, dc, :], scalar=st[:, 14:15], in1=pk[:, 0:257], op0=ALU.mult, op1=ALU.add),
                         reads=[self.Cr, self.str_, pkr], writes=[self.Cr])
                c.op("act", lambda: nc.scalar.copy(out=self.CnTb[:], in_=self.CnT[:]), reads=[self.Cr], writes=[self.Cbr])
                c.op("dve", lambda: nc.vector.tensor_copy(out=mc, in_=st[:, 16:17]), reads=[self.str_], writes=[self.mr])
            if st_out is not None:
                c.dma("act", st_out["CT"][d, h].rearrange("(a p) e -> p a e", p=128), self.CnT[:, :, 0:256], reads=[self.Cr])
                for dc in range(2):
                    c.dma("act", st_out["n"][d, h, dc * 128:(dc + 1) * 128].rearrange("(p a) -> p a", a=1), self.CnT[:, dc, 256:257], reads=[self.Cr])
                c.dma("act", st_out["m"][d * 4 + h: d * 4 + h + 1].rearrange("(p a) -> p a", a=1), self.mcol[0:1, d * 4 + h: d * 4 + h + 1], reads=[self.mr])
        for ct in range(nch):
            t0 = ct * 128
            c.op("dve", lambda: nc.vector.bn_stats(out=self.bs[:, 0:6], in_=self.H[:, ct, :]), reads=[self.Hr], writes=[self.bsr])
            c.op("dve", lambda: nc.vector.bn_aggr(out=self.bs[:, 6:8], in_=self.bs[:, 0:6]), reads=[self.bsr], writes=[self.bsr])
            c.op("dve", lambda: nc.vector.tensor_scalar(out=self.bs[:, 7:8], in0=self.bs[:, 7:8], scalar1=1e-5, scalar2=None, op0=ALU.add), reads=[self.bsr], writes=[self.bsr])
            c.op("act", lambda: nc.scalar.activation(out=self.bs[:, 7:8], in_=self.bs[:, 7:8], func=AF.Sqrt), reads=[self.bsr], writes=[self.bsr])
            c.op("dve", lambda: nc.vector.reciprocal(out=self.bs[:, 7:8], in_=self.bs[:, 7:8]), reads=[self.bsr], writes=[self.bsr])
            c.op("dve", lambda: nc.vector.tensor_scalar(out=self.hn[:], in0=self.H[:, ct, :], scalar1=self.bs[:, 6:7], scalar2=self.bs[:, 7:8], op0=ALU.subtract, op1=ALU.mult),
                 reads=[self.Hr, self.bsr], writes=[self.hnr])
            for ec in range(2):
                c.op("pe", lambda ec=ec: nc.tensor.transpose(out=self.pbf[:, ec * 128:(ec + 1) * 128], in_=self.hn[:, ec * 128:(ec + 1) * 128], identity=self.Kb[:, 0, :]),
                     reads=[self.hnr, self.Kr], writes=[self.pbfr])
            for ec in range(2):
                c.op("dve", lambda ec=ec: nc.vector.scalar_tensor_tensor(out=self.ob[:, ec, t0:t0 + 128], in0=self.pbf[:, ec * 128:(ec + 1) * 128], scalar=self.nw[:, h * 2 + ec: h * 2 + ec + 1],
                                                                       in1=self.sigo[:, ec, t0:t0 + 128], op0=ALU.mult, op1=ALU.mult), reads=[self.pbfr, self.pr, self.sigor], writes=[self.obr])
        c.dma("act", BRD.ap[br0 + h * 2: br0 + h * 2 + 2, :, tok0:tok0 + L].rearrange("c p t -> p c t"), self.ob[:, :, 0:L], reads=[self.obr],
              writes=[BRD.res[br0 + h * 2], BRD.res[br0 + h * 2 + 1]])


def hyena_consts(L):
    LT = L // 128
    t = np.arange(L, dtype=np.float32) / np.float32(L)
    ang = (2.0 * np.pi * t[:, None] * np.arange(1, 17, dtype=np.float32)[None, :]).astype(np.float32)
    emb = np.concatenate([t[:, None], np.cos(ang), np.sin(ang)], axis=-1).astype(np.float32)
    tt = np.arange(L, dtype=np.float64)
    phi = np.pi / L * (np.arange(L, dtype=np.float64) + 0.5)
    ph = tt[:, None] * phi[None, :]
    F = np.concatenate([np.cos(ph), -np.sin(ph)], axis=1)
    G = np.concatenate([np.cos(ph).T, -np.sin(ph).T], axis=0) / L
    FA = F.reshape(LT, 128, 2 * LT, 128).transpose(2, 1, 0, 3)
    GA = G.reshape(2 * LT, 128, LT, 128).transpose(2, 1, 0, 3)
    return {"embT": np.ascontiguousarray(emb.T), "tn": np.ascontiguousarray(-t.reshape(LT, 128).T),
            "FA": np.ascontiguousarray(FA).astype(ml_dtypes.bfloat16), "GA": np.ascontiguousarray(GA).astype(ml_dtypes.bfloat16)}


class Hyena(MixBase):
    def __init__(self, c, es, consts_ap, Lmax):
        super().__init__(c, es, consts_ap, nps=7)
        sb, nc = self.sb, self.nc
        LT = Lmax // 128
        self.pbf = es.enter_context(nc.psum_tensor("mpsbf", [128, 1024], BF16)); self.pbfr = Res("pbf", excl=True)
        self.cw = sb("hy_cw", [128, 3, 24], F32); self.cb = sb("hy_cb", [128, 24], F32); self.bias = sb("hy_bias", [128, 2, 1024], F32)
        self.pr = Res()
        self.MA = [sb("hy_MA%d" % i, [128, 2 * LT * 128], BF16) for i in range(3)]; self.MAr = [Res() for _ in range(3)]
        self.mapos = 0
        self.Lmax = Lmax

    def alloc_apply(self):
        sb = self.sb
        Lmax = self.Lmax
        LT = Lmax // 128
        self.X = sb("hy_X", [128, Lmax + 4], F32); self.Xr = Res()
        self.xc = sb("hy_xc", [128, Lmax], F32); self.xcr = Res()
        self.T3 = sb("hy_T3", [128, 3, LT, 256], F32); self.T3r = Res()
        self.ztm = sb("hy_ztm", [128, LT, 256], BF16); self.ztmr = Res()
        self.Y = sb("hy_Y", [128, 2 * LT, 256], BF16); self.Yr = Res()
        self.Ht = [sb("hy_H%d" % i, [128, 2, 256], F32) for i in range(2)]; self.Htr = [Res(), Res()]
        self.hpos = 0
        self.tm = [sb("hy_tm%d" % i, [128, 256], F32) for i in range(4)]; self.tmr = [Res() for _ in range(4)]
        self.zb = sb("hy_zb", [128, 256], BF16); self.zbr = Res()
        self.ob = sb("hy_ob", [128, 2, Lmax], BF16); self.obr = Res()

    def load_params(self, prm):
        c = self.c
        for t, k in [(self.cw, "cw"), (self.cb, "cb"), (self.bias, "bias")]:
            c.dma("sp", t[:], prm[k], writes=[self.pr])

    def mat(self, ap, n):
        i = self.mapos; self.mapos = (i + 1) % 3
        self.c.dma("sp", self.MA[i][:, 0:n * 128], ap.rearrange("p a b -> p (a b)"), writes=[self.MAr[i]])
        return self.MA[i][:, 0:n * 128].rearrange("p (a b) -> p a b", a=n), self.MAr[i]

    def make_filters(self, fp, hc, L, HS):
        c, nc, sb = self.c, self.nc, self.sb
        LT = L // 128
        with ExitStack() as es2:
            sb2 = lambda name, shape, dt: es2.enter_context(nc.sbuf_tensor(name, shape, dt))
            embT = sb2("hf_emb", [33, L], F32); w1 = sb2("hf_w1", [33, 64], F32); w2 = sb2("hf_w2", [64, 64], F32); w3 = sb2("hf_w3", [64, 4096], F32)
            bb = sb2("hf_b", [64, 6], F32); tn = sb2("hf_tn", [128, LT], F32); fr = Res()
            f1T = sb2("hf_f1", [64, L], F32); f2T = sb2("hf_f2", [64, L], F32); f1r, f2r = Res(), Res()
            s2 = sb2("hf_s2", [64, 512], F32); s4 = sb2("hf_s4", [64, 512], F32); sr = Res()
            dec = sb2("hf_dec", [128, 512], F32); decr = Res()
            E = sb2("hf_E", [128, 512], F32); Er = Res()
            fl = sb2("hf_fl", [128, 512], F32); flr = Res()
            fb = [sb2("hf_fb%d" % i, [128, LT, 512], BF16) for i in range(2)]; fbr = [Res(), Res()]
            row = sb2("hf_row", [1, 512], F32); rowr = Res()
            SCL = sb2("hf_SCL", [128, 512], F32); SCLr = Res()
            Asb = sb2("hf_A", [128, 512], F32); Ar = Res()
            Hsb = [sb2("hf_Hs%d" % i, [128, 512], F32) for i in range(2)]; Hr = [Res(), Res()]
            ones_c = self.ONES[:, 0:1]; ones_r = self.ONES[0:1, :]
            for t, k in [(embT, hc["embT"]), (w1, fp["w1"]), (w2, fp["w2"]), (w3, fp["w3"]), (tn, hc["tn"])]:
                c.dma("sp", t[:], k, writes=[fr])
            c.dma("sp", bb[:, 0:1], fp["b1"], writes=[fr]); c.dma("sp", bb[:, 1:2], fp["b2"], writes=[fr])
            c.op("dve", lambda: nc.vector.tensor_scalar(out=bb[:, 2:4], in0=bb[:, 0:2], scalar1=0.5, scalar2=None, op0=ALU.mult), reads=[fr], writes=[fr])
            c.op("dve", lambda: nc.vector.tensor_scalar(out=bb[:, 4:6], in0=bb[:, 0:2], scalar1=0.25, scalar2=None, op0=ALU.mult), reads=[fr], writes=[fr])

            def sin_layer(w, kdim, src, dst, dstr, bi):
                for t0 in range(0, L, 512):
                    wd = min(512, L - t0)
                    ps, psr = self.psum()
                    c.op("pe", lambda: nc.tensor.matmul(ps[0:64, 0:wd], lhsT=w[0:kdim, :], rhs=src[0:kdim, t0:t0 + wd], start=True, stop=True), reads=[fr, f1r, f2r], writes=[psr])
                    c.op("act", lambda: nc.scalar.activation(out=s2[:, 0:wd], in_=ps[0:64, 0:wd], func=AF.Sin, scale=0.5, bias=bb[:, 2 + bi:3 + bi]), reads=[psr, fr], writes=[sr])
                    c.op("act", lambda: nc.scalar.activation(out=s4[:, 0:wd], in_=ps[0:64, 0:wd], func=AF.Sin, scale=0.25, bias=bb[:, 4 + bi:5 + bi]), reads=[psr, fr], writes=[sr])
                    c.op("dve", lambda: nc.vector.tensor_tensor(out=s4[:, 0:wd], in0=s4[:, 0:wd], in1=s4[:, 0:wd], op=ALU.mult), reads=[sr], writes=[sr])
                    c.op("dve", lambda: nc.vector.tensor_scalar(out=s4[:, 0:wd], in0=s4[:, 0:wd], scalar1=-2.0, scalar2=1.0, op0=ALU.mult, op1=ALU.add), reads=[sr], writes=[sr])
                    c.op("dve", lambda: nc.vector.scalar_tensor_tensor(out=dst[:, t0:t0 + wd], in0=s2[:, 0:wd], scalar=2.0, in1=s4[:, 0:wd], op0=ALU.mult, op1=ALU.mult), reads=[sr], writes=[dstr])

            sin_layer(w1, 33, embT, f1T, f1r, 0)
            sin_layer(w2, 64, f1T, f2T, f2r, 1)
            pstat, pstatr = self.ps[6], self.psr[6]
            self.nps = 6; self.pspos = 0
            for o in range(2):
                for half in range(2):
                    for d in range(2):
                        ch0 = o * 2048 + d * 1024 + half * 512
                        c.dma("sp", dec[:], fp["dec"][:, ch0:ch0 + 512], writes=[decr])
                        for tc in range(LT):
                            ps, psr = self.psum()
                            c.op("pe", lambda: nc.tensor.matmul(ps[:, 0:512], lhsT=f2T[:, tc * 128:(tc + 1) * 128], rhs=w3[:, ch0:ch0 + 512], start=True, stop=True), reads=[f2r, fr], writes=[psr])
                            c.op("act", lambda: nc.scalar.activation(out=E[:], in_=dec[:], func=AF.Exp, scale=tn[:, tc:tc + 1]), reads=[decr, fr], writes=[Er])
                            c.op("dve", lambda: nc.vector.tensor_tensor(out=fl[:], in0=ps[:, 0:512], in1=E[:], op=ALU.mult), reads=[psr, Er], writes=[flr])
                            if d == 1 and tc == 0:
                                c.op("dve", lambda: nc.vector.memset(fl[0:1, :], 0.0), writes=[flr])
                            c.op("act", lambda: nc.scalar.copy(out=fb[d][:, tc, :], in_=fl[:]), reads=[flr], writes=[fbr[d]])
                            c.op("act", lambda: nc.scalar.activation(out=fl[:], in_=fl[:], func=AF.Abs), reads=[flr], writes=[flr])
                            c.op("pe", lambda: nc.tensor.matmul(pstat[0:1, 0:512], lhsT=ones_c, rhs=fl[:], start=(d == 0 and tc == 0), stop=(d == 1 and tc == LT - 1), skip_group_check=True),
                                 reads=[flr, self.Kr], writes=[pstatr])
                    c.op("dve", lambda: nc.vector.tensor_scalar(out=row[:], in0=pstat[0:1, 0:512], scalar1=1e-6, scalar2=None, op0=ALU.add), reads=[pstatr], writes=[rowr])
                    c.op("dve", lambda: nc.vector.reciprocal(out=row[:], in_=row[:]), reads=[rowr], writes=[rowr])
                    pb, pbr = self.psum()
                    c.op("pe", lambda: nc.tensor.matmul(pb[:, 0:512], lhsT=ones_r, rhs=row[:], start=True, stop=True), reads=[rowr, self.Kr], writes=[pbr])
                    c.op("act", lambda: nc.scalar.copy(out=SCL[:], in_=pb[:, 0:512]), reads=[pbr], writes=[SCLr])
                    for fc in range(2 * LT):
                        M, Mr = self.mat(hc["FA"][fc], LT)
                        pa, par = self.psum(); pbb, pbbr = self.psum()
                        for tc in range(LT):
                            c.op("pe", lambda tc=tc: nc.tensor.matmul(pa[:, 0:512], lhsT=M[:, tc, :], rhs=fb[0][:, tc, :], start=(tc == 0), stop=(tc == LT - 1)), reads=[Mr, fbr[0]], writes=[par])
                        for tc in range(LT):
                            c.op("pe", lambda tc=tc: nc.tensor.matmul(pbb[:, 0:512], lhsT=M[:, tc, :], rhs=fb[1][:, tc, :], start=(tc == 0), stop=(tc == LT - 1)), reads=[Mr, fbr[1]], writes=[pbbr])
                        c.op("act", lambda: nc.scalar.copy(out=Asb[:], in_=pa[:, 0:512]), reads=[par], writes=[Ar])
                        hs, hsr = Hsb[fc % 2], Hr[fc % 2]
                        c.op("dve", lambda: nc.vector.tensor_tensor(out=hs[:], in0=Asb[:], in1=pbb[:, 0:512], op=(ALU.add if fc < LT else ALU.subtract)), reads=[Ar, pbbr], writes=[hsr])
                        c.op("dve", lambda: nc.vector.tensor_tensor(out=hs[:], in0=hs[:], in1=SCL[:], op=ALU.mult), reads=[hsr, SCLr], writes=[hsr])
                        c.dma("act", HS.ap[o, fc // LT, fc % LT, :, half * 512:(half + 1) * 512], hs[:], reads=[hsr], writes=[HS.res[o]])
            self.nps = 7
        c.barrier()

    def run_seq(self, FMX, u0, BRD, br0, tok0, L, hc, HS):
        c, nc = self.c, self.nc
        LT = L // 128
        for p in range(4):
            for j in range(3):
                for q in range(2):
                    ci = j * 8 + p * 2 + q
                    X, xc = self.X, self.xc
                    c.op("dve", lambda: nc.vector.memset(X[:, 0:1], 0.0), writes=[self.Xr])
                    c.op("dve", lambda: nc.vector.memset(X[:, L + 1:L + 2], 0.0), writes=[self.Xr])
                    c.dma("sp", X[:, 1:L + 1], FMX.ap[u0 + ci, :, tok0:tok0 + L], reads=[FMX.res[u0 + ci]], writes=[self.Xr])
                    c.op("dve", lambda: nc.vector.tensor_scalar(out=xc[:, 0:L], in0=X[:, 0:L], scalar1=self.cw[:, 0, ci:ci + 1], scalar2=self.cb[:, ci:ci + 1], op0=ALU.mult, op1=ALU.add),
                         reads=[self.Xr, self.pr], writes=[self.xcr])
                    for k in range(1, 3):
                        c.op("dve", lambda k=k: nc.vector.scalar_tensor_tensor(out=xc[:, 0:L], in0=X[:, k:k + L], scalar=self.cw[:, k, ci:ci + 1], in1=xc[:, 0:L], op0=ALU.mult, op1=ALU.add),
                             reads=[self.Xr, self.xcr, self.pr], writes=[self.xcr])
                    for t4 in range(0, LT, 4):
                        n4 = min(4, LT - t4)
                        ps, psr = self.psum()
                        for k in range(n4):
                            c.op("pe", lambda k=k: nc.tensor.transpose(out=ps[:, k * 128:(k + 1) * 128], in_=xc[:, (t4 + k) * 128:(t4 + k + 1) * 128], identity=self.IDENT),
                                 reads=[self.xcr, self.Kr], writes=[psr])
                        c.op("act", lambda: nc.scalar.copy(out=self.T3[:, j, t4:t4 + n4, q * 128:(q + 1) * 128], in_=ps[:, 0:n4 * 128].rearrange("p (k e) -> p k e", k=n4)),
                             reads=[psr], writes=[self.T3r])
            c.op("act", lambda: nc.scalar.copy(out=self.ztm[:, 0:LT, :], in_=self.T3[:, 0, 0:LT, :]), reads=[self.T3r], writes=[self.ztmr])
            for o in range(2):
                for fc in range(LT):
                    Mre, Mrer = self.mat(hc["FA"][fc], LT)
                    Mim, Mimr = self.mat(hc["FA"][LT + fc], LT)
                    H, Hr_ = self.Ht[self.hpos], self.Htr[self.hpos]; self.hpos ^= 1
                    c.dma("sp", H[:], HS.ap[o, :, fc, :, p * 256:(p + 1) * 256].rearrange("a p e -> p a e"), reads=[HS.res[o]], writes=[Hr_])
                    pr_, prr = self.psum(); pi_, pir = self.psum()
                    for tc in range(LT):
                        c.op("pe", lambda tc=tc: nc.tensor.matmul(pr_[:, 0:256], lhsT=Mre[:, tc, :], rhs=self.ztm[:, tc, :], start=(tc == 0), stop=(tc == LT - 1)), reads=[Mrer, self.ztmr], writes=[prr])
                    for tc in range(LT):
                        c.op("pe", lambda tc=tc: nc.tensor.matmul(pi_[:, 0:256], lhsT=Mim[:, tc, :], rhs=self.ztm[:, tc, :], start=(tc == 0), stop=(tc == LT - 1)), reads=[Mimr, self.ztmr], writes=[pir])
                    tm, tmr = self.tm, self.tmr
                    c.op("dve", lambda: nc.vector.tensor_tensor(out=tm[0][:], in0=pr_[:, 0:256], in1=H[:, 0, :], op=ALU.mult), reads=[prr, Hr_], writes=[tmr[0]])
                    c.op("dve", lambda: nc.vector.tensor_tensor(out=tm[1][:], in0=pi_[:, 0:256], in1=H[:, 1, :], op=ALU.mult), reads=[pir, Hr_], writes=[tmr[1]])
                    c.op("dve", lambda: nc.vector.tensor_tensor(out=tm[2][:], in0=pr_[:, 0:256], in1=H[:, 1, :], op=ALU.mult), reads=[prr, Hr_], writes=[tmr[2]])
                    c.op("dve", lambda: nc.vector.tensor_tensor(out=tm[3][:], in0=pi_[:, 0:256], in1=H[:, 0, :], op=ALU.mult), reads=[pir, Hr_], writes=[tmr[3]])
                    c.op("pool", lambda: nc.gpsimd.tensor_tensor(out=self.Y[:, fc, :], in0=tm[0][:], in1=tm[1][:], op=ALU.subtract), reads=[tmr[0], tmr[1]], writes=[self.Yr])
                    c.op("pool", lambda: nc.gpsimd.tensor_tensor(out=self.Y[:, LT + fc, :], in0=tm[2][:], in1=tm[3][:], op=ALU.add), reads=[tmr[2], tmr[3]], writes=[self.Yr])
                for tco in range(LT):
                    Mg, Mgr = self.mat(hc["GA"][tco], 2 * LT)
                    pz, pzr = self.psum()
                    for fc in range(2 * LT):
                        c.op("pe", lambda fc=fc: nc.tensor.matmul(pz[:, 0:256], lhsT=Mg[:, fc, :], rhs=self.Y[:, fc, :], start=(fc == 0), stop=(fc == 2 * LT - 1)), reads=[Mgr, self.Yr], writes=[pzr])
                    z = self.T3[:, 0, tco, :]
                    c.op("dve", lambda: nc.vector.tensor_tensor(out=z, in0=z, in1=self.bias[:, o, p * 256:(p + 1) * 256], op=ALU.mult), reads=[self.T3r, self.pr], writes=[self.T3r])
                    c.op("dve", lambda: nc.vector.tensor_tensor(out=z, in0=z, in1=pz[:, 0:256], op=ALU.add), reads=[self.T3r, pzr], writes=[self.T3r])
                    c.op("dve", lambda: nc.vector.tensor_tensor(out=z, in0=z, in1=self.T3[:, 1 + o, tco, :], op=ALU.mult), reads=[self.T3r], writes=[self.T3r])
                    if o == 0:
                        c.op("act", lambda: nc.scalar.copy(out=self.ztm[:, tco, :], in_=z), reads=[self.T3r], writes=[self.ztmr])
                    else:
                        c.op("act", lambda: nc.scalar.copy(out=self.zb[:], in_=z), reads=[self.T3r], writes=[self.zbr])
                        for q in range(2):
                            c.op("pe", lambda q=q: nc.tensor.transpose(out=self.pbf[:, q * 128:(q + 1) * 128], in_=self.zb[:, q * 128:(q + 1) * 128], identity=self.Kb[:, 0, :]),
                                 reads=[self.zbr, self.Kr], writes=[self.pbfr])
                        c.op("act", lambda: nc.scalar.copy(out=self.ob[:, :, tco * 128:(tco + 1) * 128], in_=self.pbf[:, 0:256].rearrange("p (q t) -> p q t", q=2)), reads=[self.pbfr], writes=[self.obr])
            c.dma("act", BRD.ap[br0 + 2 * p: br0 + 2 * p + 2, :, tok0:tok0 + L].rearrange("c p t -> p c t"), self.ob[:, :, 0:L], reads=[self.obr],
                  writes=[BRD.res[br0 + 2 * p], BRD.res[br0 + 2 * p + 1]])


NTOK = 3072
NT = 6
LS = 2048
LP = 256
SEQS = [(0, LS, True)] + [(LS + LP * j, LP, False) for j in range(4)]
C_RGX, C_RGY, C_HYU, C_XBC, C_O = 0, 8, 16, 40, 52
FM_SPECS = [(0, 5120, 0), (6144, 1536, 40), (10784, 1024, 52)]
TM_SPECS = [(5120, 1024, "z", 0), (7680, 32, "dtif", 0), (7712, 3072, "qkv", 0), (11808, 16, "dtif", 32)]
MERGE_COL0 = 11824


class _UniqNC:
    def __init__(self, nc):
        object.__setattr__(self, "_nc", nc)
        object.__setattr__(self, "_n", 0)

    def __getattr__(self, k):
        return getattr(self._nc, k)

    def _uniq(self, name):
        object.__setattr__(self, "_n", self._n + 1)
        return "%s_u%d" % (name, self._n)

    def sbuf_tensor(self, name, shape, dt):
        return self._nc.sbuf_tensor(self._uniq(name), shape, dt)

    def psum_tensor(self, name, shape, dt):
        return self._nc.psum_tensor(self._uniq(name), shape, dt)

    def semaphore(self, name):
        return self._nc.semaphore(self._uniq(name))


def build_program():
    cfg = Cfg()
    KC, T = cfg.KC, cfg.T
    nc = _UniqNC(bass.Bass("TRN2", target_bir_lowering=False))
    D_ = {}

    def di(name, shape, dt=F32):
        D_[name] = nc.dram_tensor(name, list(shape), dt, kind="ExternalInput").ap()
        return D_[name]

    def do(name, shape, dt=F32):
        return nc.dram_tensor(name, list(shape), dt, kind="ExternalOutput").ap()

    def ds(name, shape, dt=F32):
        return nc.dram_tensor(name, list(shape), dt, kind="Internal").ap()

    xin = di("xT", [NT, KC, 128, T]); yout = do("yT", [NT, KC, 128, T])
    ada_w = di("ada_w", [2, 4096, 36864]); wg = di("ffn_wg", [2, 2, 4096, 8192]); wu = di("ffn_wu", [2, 2, 4096, 8192]); wd = di("ffn_wd", [2, 2, 8192, 4096])
    w_in = di("w_in", [2, 4096, 28208]); branch_w = di("branch_w", [2, 4, 1024, 4096]); mix_out = di("mix_out", [2, 4096, 4096])
    cT = di("cT", [128, KC, 2]); adab = di("adab", [2, 128, 288]); lng = di("lng", [2, 128, 3, 32]); lnb = di("lnb", [2, 128, 3, 32])
    kc = di("kc", [128, 8, 128])
    rg_p = {"cw": di("rg_cw", [2, 128, 4, 8]), "cb": di("rg_cb", [2, 128, 8]), "gb": di("rg_gb", [2, 128, 4, 8]), "lam": di("rg_lam", [2, 128, 2, 8]), "gw": di("rg_gate_w", [2, 2, 2, 8, 128, 128])}
    sd_p = {"cw": di("sd_cw", [2, 128, 4, 12]), "cb": di("sd_cb", [2, 128, 12]), "dtb": di("sd_dtb", [2, 128, 32]), "alog": di("sd_alog", [2, 128, 32]), "D": di("sd_D", [2, 128, 16]), "nw": di("sd_nw", [2, 128, 8])}
    ml_p = {"gbias": di("ml_gbias", [2, 128, 16]), "nw": di("ml_nw", [2, 128, 8])}
    hy_p = {"cw": di("hy_cw", [2, 128, 3, 24]), "cb": di("hy_cb", [2, 128, 24]), "bias": di("hy_bias", [2, 128, 2, 1024])}
    hy_f = {"w1": di("hy_w1", [2, 33, 64]), "b1": di("hy_b1", [2, 64, 1]), "w2": di("hy_w2", [2, 64, 64]), "b2": di("hy_b2", [2, 64, 1]), "w3": di("hy_w3", [2, 64, 4096]), "dec": di("hy_dec", [2, 128, 4096])}
    hcs = {}
    for L in (LS, LP):
        LT = L // 128
        hcs[L] = {"embT": di("embT%d" % L, [33, L]), "tn": di("tn%d" % L, [128, LT]), "FA": di("FA%d" % L, [2 * LT, 128, LT, 128], BF16), "GA": di("GA%d" % L, [LT, 128, 2 * LT, 128], BF16)}
    rope = (di("rope_cos", [LS, 128]), di("rope_sin", [LS, 128]))
    st_rg = di("st_rg", [2, 2, 8, 128]); st_ssd = di("st_ssd", [2, 2, 128, 16, 64]); st_C = di("st_C", [2, 2, 4, 256, 256]); st_n = di("st_n", [2, 2, 4, 256]); st_m = di("st_m", [2, 128, 8])
    o_rg = do("o_rg", [4, 2, 2, 8, 128]); o_ssd = do("o_ssd", [4, 2, 2, 128, 16, 64]); o_C = do("o_C", [4, 2, 2, 4, 256, 256]); o_n = do("o_n", [4, 2, 2, 4, 256]); o_m = do("o_m", [4, 2, 8])
    X1 = [DT(ds("X1_%d" % t, [KC, 128, T])) for t in range(NT)]
    X2 = [DT(ds("X2_%d" % t, [KC, 128, T])) for t in range(NT)]
    X3 = [DT(ds("X3_%d" % t, [KC, 128, T])) for t in range(NT)]
    XIN = [DT(xin[t]) for t in range(NT)]
    YOUT = [DT(yout[t]) for t in range(NT)]
    YPRE = DT(ds("YPRE", [KC, 128, T]))
    H2D = DT(ds("H2D", [NT, 128, KC, T], BF16))
    FMX = DT(ds("FMX", [60, 128, NTOK]))
    TM = {"z": DT(ds("ZTM", [NTOK, 1024]), 1), "dtif": DT(ds("DTIF", [NTOK, 48]), 1), "qkv": DT(ds("QKV", [NTOK, 3072]), 1)}
    BRD = DT(ds("BRD", [32, 128, NTOK], BF16))
    HS = {L: DT(ds("HS%d" % L, [2, 2, L // 128, 128, 1024])) for L in (LS, LP)}

    with ExitStack() as es0:
        c = Ctx(nc, es0)
        sb0 = lambda name, shape, dt: es0.enter_context(nc.sbuf_tensor(name, shape, dt))
        modT = [sb0("modT%d" % l, [128, 9 * KC, 2], F32) for l in range(2)]; modr = Res("mod")
        LNG = sb0("LNG", [128, 2, 3, 32], F32); LNB = sb0("LNB", [128, 2, 3, 32], F32)
        for l in range(2):
            c.dma("sp", LNG[:, l], lng[l], writes=[modr]); c.dma("sp", LNB[:, l], lnb[l], writes=[modr])
        with ExitStack() as es1:
            d = Dense(c, cfg, es1)
            d.constr = modr
            sb1 = lambda name, shape, dt: es1.enter_context(nc.sbuf_tensor(name, shape, dt))
            ct = sb1("cT", [128, KC, 2], F32); cb = sb1("cTb", [128, KC, 2], BF16); ab = sb1("adab", [128, 288], F32); cr = Res()
            c.dma("sp", ct[:], cT, writes=[cr])
            c.op("act", lambda: nc.scalar.activation(out=cb[:], in_=ct[:], func=AF.Silu), reads=[cr], writes=[cr])
            for l in range(2):
                c.dma("sp", ab[:], adab[l], writes=[modr])
                d.adaln(ada_w[l], ab, cb, cr, modT[l], modr)
            c.barrier()

        def mcol(l, j, s):
            return lambda n: modT[l][:, j * KC + n, s:s + 1]

        for l in range(2):
            xsrc = XIN if l == 0 else X3
            xdst = X3 if l == 0 else YOUT
            with ExitStack() as esA:
                d = Dense(c, cfg, esA); d.constr = modr
                c.op("dve", lambda: nc.vector.memset(d.ones_c[:], 1.0), writes=[modr])
                for t in range(NT):
                    s = 1 if t < 4 else 0
                    d.load_h(xsrc[t], mcol(l, 1, s), mcol(l, 0, s))
                    d.ffn_up(wg[l, 0], wu[l, 0])
                    d.proj_res(wd[l, 0], d.HID, d.HIDr, cfg.FC, xsrc[t], mcol(l, 2, s), YPRE)
                    d.ln_pass(YPRE, lambda n: LNG[:, l, 0, n:n + 1], lambda n: LNB[:, l, 0, n:n + 1], X1[t], mcol(l, 4, s), mcol(l, 3, s))
                    c.dma("act", H2D.ap[t], d.HB[:], reads=[d.HBr], writes=[H2D.res[t]])
                    d.proj_mixer(w_in[l], [(c0, n, FMX, ch0) for (c0, n, ch0) in FM_SPECS], [(c0, n, TM[k], dc0) for (c0, n, k, dc0) in TM_SPECS], t * T)
                c.barrier()
            with ExitStack() as esB:
                m = RgLru(c, esB, kc, LS)
                m.load_params({k: v[l] for k, v in rg_p.items()})
                for j, (tok0, L, lat) in enumerate(SEQS):
                    m.run_seq(FMX, BRD, tok0, L, st_rg[l] if lat else None, None if lat else o_rg[j - 1, l])
                c.barrier()
            with ExitStack() as esB:
                m = Hyena(c, esB, kc, LS)
                m.load_params({k: v[l] for k, v in hy_p.items()})
                for L in (LS, LP):
                    m.make_filters({k: v[l] for k, v in hy_f.items()}, hcs[L], L, HS[L])
                m.alloc_apply()
                for j, (tok0, L, lat) in enumerate(SEQS):
                    m.run_seq(FMX, C_HYU, BRD, 8, tok0, L, hcs[L], HS[L])
                c.barrier()
            with ExitStack() as esB:
                m = Ssd(c, esB, kc, LS)
                m.load_params({k: v[l] for k, v in sd_p.items()})
                for j, (tok0, L, lat) in enumerate(SEQS):
                    m.run_seq(FMX, C_XBC, TM["z"], TM["dtif"], BRD, 16, tok0, L, st_ssd[l] if lat else None, None if lat else o_ssd[j - 1, l])
                c.barrier()
            with ExitStack() as esB:
                m = Mlstm(c, esB, kc, LS, True)
                m.load_params({k: v[l] for k, v in ml_p.items()})
                for j, (tok0, L, lat) in enumerate(SEQS):
                    sti = {"C0T": st_C[l], "n0": st_n[l], "m0rep": st_m[l]} if lat else None
                    sto = None if lat else {"CT": o_C[j - 1, l], "n": o_n[j - 1, l], "m": o_m[j - 1, l]}
                    m.run_seq(TM["qkv"], TM["dtif"], FMX, C_O, BRD, 24, tok0, L, sti, sto, rope if lat else None)
                c.barrier()
            with ExitStack() as esC:
                d = Dense(c, cfg, esC); d.constr = modr
                c.op("dve", lambda: nc.vector.memset(d.ones_c[:], 1.0), writes=[modr])
                BR = d.HID[:, KC:2 * KC, :]; MERGED = d.HID[:, 0:KC, :]; MERGEDr = Res("merged")
                for t in range(NT):
                    s = 1 if t < 4 else 0
                    c.dma("sp", d.HB[:], H2D.ap[t], reads=[H2D.res[t]], writes=[d.HBr])
                    c.dma("sp", BR, BRD.ap[:, :, t * T:(t + 1) * T].rearrange("c p t -> p c t"), reads=BRD.res, writes=[d.HIDr])
                    d.merge(w_in[l], MERGE_COL0, branch_w[l], BR, d.HIDr, MERGED, MERGEDr)
                    d.proj_res(mix_out[l], MERGED, MERGEDr, KC, X1[t], mcol(l, 5, s), YPRE)
                    d.ln_pass(YPRE, lambda n: LNG[:, l, 1, n:n + 1], lambda n: LNB[:, l, 1, n:n + 1], X2[t], mcol(l, 7, s), mcol(l, 6, s))
                    d.ffn_up(wg[l, 1], wu[l, 1])
                    d.proj_res(wd[l, 1], d.HID, d.HIDr, cfg.FC, X2[t], mcol(l, 8, s), YPRE)
                    d.ln_pass(YPRE, lambda n: LNG[:, l, 2, n:n + 1], lambda n: LNB[:, l, 2, n:n + 1], xdst[t], None, None)
                c.barrier()
        c.final()
        print("instructions:", c.n_ins)
    return nc._nc


def _layout_inputs(inp):
    f32 = np.float32
    g = lambda k: np.asarray(inp[k], dtype=f32)
    xp, xs = g("x_prompt"), g("x_sample")
    shared = {k: g(k) for k in ["ada_w", "ffn_wg", "ffn_wu", "ffn_wd", "w_in", "branch_w", "mix_out", "rg_gate_w", "hy_w1", "hy_w2", "hy_w3"]}
    shared["adab"] = np.stack([fm(g("ada_b")[l]) for l in range(2)])
    shared["lng"] = np.stack([fm(g("ln_g")[l]) for l in range(2)]); shared["lnb"] = np.stack([fm(g("ln_b")[l]) for l in range(2)])
    shared["kc"] = mixer_consts()
    shared["rg_cw"] = np.stack([fm(g("rg_conv_w")[l]) for l in range(2)]); shared["rg_cb"] = np.stack([fm(g("rg_conv_b")[l]) for l in range(2)])
    shared["rg_gb"] = np.stack([fm(g("rg_gate_b")[l].reshape(4, 1024)) for l in range(2)]); shared["rg_lam"] = np.stack([fm(g("rg_lambda")[l]) for l in range(2)])
    shared["sd_cw"] = np.stack([fm(g("ssd_conv_w")[l]) for l in range(2)]); shared["sd_cb"] = np.stack([fm(g("ssd_conv_b")[l]) for l in range(2)])
    shared["sd_dtb"] = np.stack([rep(g("ssd_dt_bias")[l]) for l in range(2)]); shared["sd_alog"] = np.stack([rep(g("ssd_A_log")[l]) for l in range(2)])
    shared["sd_D"] = np.stack([rep(g("ssd_D")[l]) for l in range(2)]); shared["sd_nw"] = np.stack([fm(g("ssd_norm_w")[l]) for l in range(2)])
    shared["ml_gbias"] = np.stack([rep(g("ml_gate_b")[l]) for l in range(2)]); shared["ml_nw"] = np.stack([fm(g("ml_norm_w")[l]) for l in range(2)])
    shared["hy_cw"] = np.stack([fm(g("hy_conv_w")[l]) for l in range(2)]); shared["hy_cb"] = np.stack([fm(g("hy_conv_b")[l]) for l in range(2)])
    shared["hy_bias"] = np.stack([rep(g("hy_bias")[l]).reshape(128, 2, 1024) for l in range(2)])
    shared["hy_b1"] = g("hy_b1").reshape(2, 64, 1); shared["hy_b2"] = g("hy_b2").reshape(2, 64, 1)
    shared["hy_dec"] = np.stack([rep(g("hy_decay")[l]) for l in range(2)])
    for L in (LS, LP):
        for k, v in hyena_consts(L).items():
            shared["%s%d" % (k, L)] = v
    pos = np.arange(LS)
    freqs = (10000.0 ** (-np.arange(64, dtype=f32) / f32(64))).astype(f32)
    ang = np.concatenate([(pos // 64).astype(f32)[:, None] * freqs[None], (pos % 64).astype(f32)[:, None] * freqs[None]], 1).astype(f32)
    shared["rope_cos"] = np.cos(ang).astype(f32); shared["rope_sin"] = np.sin(ang).astype(f32)
    shared = {k: np.ascontiguousarray(v) for k, v in shared.items()}
    per_b = []
    for b in range(2):
        per_b.append({
            "st_rg": np.ascontiguousarray(g("state_rglru")[b].reshape(2, 2, 8, 128)),
            "st_ssd": np.ascontiguousarray(g("state_ssd")[b].transpose(0, 1, 4, 2, 3)),
            "st_C": np.ascontiguousarray(g("state_mlstm_C")[b].transpose(0, 1, 2, 4, 3)),
            "st_n": np.ascontiguousarray(g("state_mlstm_n")[b]),
            "st_m": np.stack([rep(g("state_mlstm_m")[b, l]) for l in range(2)]),
        })
    in_maps = []
    for core in range(8):
        b = core % 2
        toks = np.concatenate([xs[b], xp[4 * core:4 * core + 4].reshape(4 * LP, 4096)], 0)
        xT = np.ascontiguousarray(toks.reshape(NT, 512, 32, 128).transpose(0, 2, 3, 1))
        cvec = np.stack([g("c_ctx"), g("c")[b]], -1)
        m = dict(shared)
        m.update(per_b[b])
        m["xT"] = xT
        m["cT"] = np.ascontiguousarray(cvec.reshape(32, 128, 2).transpose(1, 0, 2))
        in_maps.append(m)
    return in_maps


_NC_CACHE = {}


def kernel(**inputs):
    in_maps = _layout_inputs(inputs)
    if "nc" not in _NC_CACHE:
        _NC_CACHE["nc"] = build_program()
    nc = _NC_CACHE["nc"]
    res = run_bass_kernel_spmd(nc, in_maps, core_ids=list(range(8)))
    R = res.results
    f32 = np.float32
    y_prompt = np.zeros((32, LP, 4096), f32); y_sample = np.zeros((2, LS, 4096), f32)
    o_rg = np.zeros((32, 2, 2, 1024), f32); o_ssd = np.zeros((32, 2, 2, 16, 64, 128), f32)
    o_C = np.zeros((32, 2, 2, 4, 256, 256), f32); o_n = np.zeros((32, 2, 2, 4, 256), f32); o_m = np.zeros((32, 2, 2, 4), f32)
    for core in range(8):
        r = R[core]
        toks = np.asarray(r["yT"]).transpose(0, 3, 1, 2).reshape(NTOK, 4096)
        if core < 2:
            y_sample[core] = toks[:LS]
        y_prompt[4 * core:4 * core + 4] = toks[LS:].reshape(4, LP, 4096)
        o_rg[4 * core:4 * core + 4] = np.asarray(r["o_rg"]).reshape(4, 2, 2, 1024)
        o_ssd[4 * core:4 * core + 4] = np.asarray(r["o_ssd"]).transpose(0, 1, 2, 4, 5, 3)
        o_C[4 * core:4 * core + 4] = np.asarray(r["o_C"]).transpose(0, 1, 2, 3, 5, 4)
        o_n[4 * core:4 * core + 4] = np.asarray(r["o_n"])
        o_m[4 * core:4 * core + 4] = np.asarray(r["o_m"]).reshape(4, 2, 2, 4)
    return (y_prompt, y_sample, o_rg, o_ssd, o_C, o_n, o_m)
```
